# Optimizing a Trainium2 kernel written in Bass

```python
import jax, jax.numpy as jnp
from jax import lax
import numpy as np

D_MODEL = 1024
BATCH = 8
SEQ = 2048
DEPTH = 4

N_REC_LAYERS = (DEPTH + 1) // 2
N_ATT_LAYERS = DEPTH // 2
D_A = D_MODEL // 2
A_HEADS = 4
A_BLOCK = D_A // A_HEADS
CONV_WIDTH = 4
CONV_PAD = (2, 1)
RGLRU_C = 8.0
D_B = D_MODEL // 2
B_HEADS = 4
B_DK = D_B // B_HEADS
B_DV = D_B // B_HEADS
CHUNK = 32
REC_IN = 2 * D_A + 5 * D_B
REC_SPLITS = (D_A, 2 * D_A, 2 * D_A + D_B, 2 * D_A + 2 * D_B, 2 * D_A + 3 * D_B, 2 * D_A + 4 * D_B)
HEAD_DIM = 64
N_Q_HEADS = D_MODEL // HEAD_DIM
N_KV_HEADS = 4
GROUP = N_Q_HEADS // N_KV_HEADS
WINDOW = 128
QBLOCK = 128
ROPE_THETA = 10000.0
QKV_OUT = (N_Q_HEADS + 2 * N_KV_HEADS) * HEAD_DIM
D_FF = 4 * D_MODEL
EPS = 1e-6

kernel_name = "hybrid_rglru_hgrn2_swa_encoder"


def rmsnorm(x, g):
    xf = x.astype(jnp.float32)
    y = xf * lax.rsqrt(jnp.mean(xf * xf, axis=-1, keepdims=True) + EPS) * g.astype(jnp.float32)
    return y.astype(x.dtype)


def dwconv(x, w, b):
    y = lax.conv_general_dilated(x, w[:, None, :].astype(x.dtype), window_strides=(1,),
                                 padding=[CONV_PAD], dimension_numbers=('NWC', 'WIO', 'NWC'),
                                 feature_group_count=x.shape[-1])
    return y + b.astype(x.dtype)


def block_diag(x, w, b):
    xh = x.reshape(x.shape[:-1] + (A_HEADS, A_BLOCK))
    return jnp.einsum('bshi,hij->bshj', xh, w.astype(jnp.float32)).reshape(x.shape) + b.astype(jnp.float32)


def rglru(xc, w_r, b_r, w_i, b_i, lam, reverse):
    r = jax.nn.sigmoid(block_diag(xc, w_r, b_r))
    i = jax.nn.sigmoid(block_diag(xc, w_i, b_i))
    log_a = -RGLRU_C * r * jax.nn.softplus(-lam.astype(jnp.float32))
    a = jnp.exp(log_a)
    u = jnp.sqrt(-jnp.expm1(2.0 * log_a)) * (i * xc)

    def combine(left, right):
        a1, b1 = left
        a2, b2 = right
        return a1 * a2, a2 * b1 + b2

    _, h = lax.associative_scan(combine, (a, u), axis=1, reverse=reverse)
    return h


def hgrn2_direction(q, log_f, v):
    bsz, s = q.shape[:2]
    n = s // CHUNK

    def chunks(t):
        return t.reshape(bsz, n, CHUNK, B_HEADS, t.shape[-1]).transpose(0, 3, 1, 2, 4)

    q, log_f, v = chunks(q), chunks(log_f), chunks(v)
    k = -jnp.expm1(log_f)
    bcum = jnp.cumsum(log_f, axis=3)
    b_last = bcum[:, :, :, -1:, :]
    qe = q * jnp.exp(bcum)
    ke = k * jnp.exp(-bcum)
    tri = jnp.tril(jnp.ones((CHUNK, CHUNK), dtype=bool))
    att = jnp.where(tri, jnp.einsum('bhncd,bhned->bhnce', qe, ke), 0.0)
    o = jnp.einsum('bhnce,bhnev->bhncv', att, v)
    kd = k * jnp.exp(b_last - bcum)
    upd = jnp.einsum('bhncd,bhncv->nbhdv', kd, v)
    dec = jnp.exp(b_last[:, :, :, 0, :]).transpose(2, 0, 1, 3)

    def step(state, inp):
        d, u = inp
        return d[..., None] * state + u, state

    init = jnp.zeros((bsz, B_HEADS, B_DK, B_DV), q.dtype)
    _, s_prev = lax.scan(step, init, (dec, upd))
    o = o + jnp.einsum('bhncd,nbhdv->bhncv', qe, s_prev)
    return o.transpose(0, 2, 3, 1, 4).reshape(bsz, s, B_HEADS, B_DV)


def rec_mixer(u, w_in, conv_w, conv_b, w_r, b_r, w_i, b_i, lam, lb, norm_g, w_out):
    bsz, s, _ = u.shape
    proj = u @ w_in
    xa, ga, q, zf, zb, iv, g = jnp.split(proj, REC_SPLITS, axis=-1)
    xc = dwconv(xa, conv_w, conv_b).astype(jnp.float32)
    y_rec = (rglru(xc, w_r[0], b_r[0], w_i[0], b_i[0], lam[0], False)
             + rglru(xc, w_r[1], b_r[1], w_i[1], b_i[1], lam[1], True))
    y_a = jax.nn.gelu(ga.astype(jnp.float32)) * y_rec
    heads = lambda t: t.astype(jnp.float32).reshape(bsz, s, B_HEADS, -1)
    lbf = lb.astype(jnp.float32)
    log_ff = jnp.log(lbf[0] + (1.0 - lbf[0]) * jax.nn.sigmoid(zf.astype(jnp.float32)))
    log_fb = jnp.log(lbf[1] + (1.0 - lbf[1]) * jax.nn.sigmoid(zb.astype(jnp.float32)))
    qh, vh = heads(q), heads(iv)
    flip = lambda t: jnp.flip(t, axis=1)
    o = (hgrn2_direction(qh, heads(log_ff), vh)
         + flip(hgrn2_direction(flip(qh), flip(heads(log_fb)), flip(vh))))
    o = o * lax.rsqrt(jnp.mean(o * o, axis=-1, keepdims=True) + EPS)
    o = o * norm_g.astype(jnp.float32).reshape(B_HEADS, B_DV)
    y_b = o.reshape(bsz, s, D_B) * jax.nn.silu(g.astype(jnp.float32))
    y = jnp.concatenate([y_a, y_b], axis=-1).astype(u.dtype)
    return y @ w_out


def rope(t, cos, sin):
    half = t.shape[-1] // 2
    t1, t2 = t[..., :half], t[..., half:]
    return jnp.concatenate([t1 * cos - t2 * sin, t2 * cos + t1 * sin], axis=-1)


def window_attention(u, w_qkv, sinks, w_o, cos, sin, band_mask):
    bsz, s, _ = u.shape
    nb = s // QBLOCK
    qkv = u @ w_qkv
    q, k, v = jnp.split(qkv, (N_Q_HEADS * HEAD_DIM, (N_Q_HEADS + N_KV_HEADS) * HEAD_DIM), axis=-1)
    q = rope(q.reshape(bsz, s, N_Q_HEADS, HEAD_DIM), cos, sin)
    k = rope(k.reshape(bsz, s, N_KV_HEADS, HEAD_DIM), cos, sin)
    v = v.reshape(bsz, s, N_KV_HEADS, HEAD_DIM)

    def band(t):
        tp = jnp.pad(t, ((0, 0), (QBLOCK, QBLOCK), (0, 0), (0, 0)))
        tp = tp.reshape(bsz, nb + 2, QBLOCK, N_KV_HEADS, HEAD_DIM)
        return jnp.concatenate([tp[:, :-2], tp[:, 1:-1], tp[:, 2:]], axis=2)

    kb, vb = band(k), band(v)
    qb = q.reshape(bsz, nb, QBLOCK, N_KV_HEADS, GROUP, HEAD_DIM)
    sc = jnp.einsum('bnqhgd,bnkhd->bnhgqk', qb, kb).astype(jnp.float32) * (HEAD_DIM ** -0.5)
    sc = jnp.where(band_mask[None, :, None, None], sc, -jnp.inf)
    sink = jnp.broadcast_to(sinks.astype(jnp.float32).reshape(1, 1, N_KV_HEADS, GROUP, 1, 1),
                            sc.shape[:-1] + (1,))
    p = jax.nn.softmax(jnp.concatenate([sc, sink], axis=-1), axis=-1)[..., :-1]
    o = jnp.einsum('bnhgqk,bnkhd->bnqhgd', p.astype(vb.dtype), vb)
    return o.reshape(bsz, s, N_Q_HEADS * HEAD_DIM) @ w_o


def sqrelu_mlp(u, w1, w2):
    hdn = jnp.square(jax.nn.relu(u @ w1))
    return hdn @ w2


def setup_inputs(seed: int = 0) -> dict:
    key = jax.random.key(seed)
    ks = jax.random.split(key, 20)
    nrm = lambda k, shape, scale: jax.random.normal(k, shape, jnp.float32) * scale
    u = jax.random.uniform(ks[10], (N_REC_LAYERS, 2, D_A), jnp.float32, 0.9, 0.999)
    a0 = u ** (1.0 / RGLRU_C)
    rg_lambda = jnp.log(a0) - jnp.log1p(-a0)
    return {
        "x": nrm(ks[0], (BATCH, SEQ, D_MODEL), 1.0),
        "norm_g": 1.0 + nrm(ks[1], (DEPTH, 4, D_MODEL), 0.02),
        "rec_w_in": nrm(ks[2], (N_REC_LAYERS, D_MODEL, REC_IN), D_MODEL ** -0.5),
        "rec_conv_w": nrm(ks[3], (N_REC_LAYERS, CONV_WIDTH, D_A), CONV_WIDTH ** -0.5),
        "rec_conv_b": nrm(ks[4], (N_REC_LAYERS, D_A), 0.01),
        "rg_w_r": nrm(ks[5], (N_REC_LAYERS, 2, A_HEADS, A_BLOCK, A_BLOCK), A_BLOCK ** -0.5),
        "rg_b_r": nrm(ks[6], (N_REC_LAYERS, 2, D_A), 0.01),
        "rg_w_i": nrm(ks[7], (N_REC_LAYERS, 2, A_HEADS, A_BLOCK, A_BLOCK), A_BLOCK ** -0.5),
        "rg_b_i": nrm(ks[8], (N_REC_LAYERS, 2, D_A), 0.01),
        "rg_lambda": rg_lambda,
        "hgrn_lb_logits": nrm(ks[11], (2, N_REC_LAYERS, D_B), 0.5),
        "hgrn_norm_g": 1.0 + nrm(ks[12], (N_REC_LAYERS, D_B), 0.02),
        "rec_w_out": nrm(ks[13], (N_REC_LAYERS, D_A + D_B, D_MODEL), (D_A + D_B) ** -0.5),
        "att_w_qkv": nrm(ks[14], (N_ATT_LAYERS, D_MODEL, QKV_OUT), D_MODEL ** -0.5),
        "att_sinks": nrm(ks[15], (N_ATT_LAYERS, N_Q_HEADS), 0.5),
        "att_w_o": nrm(ks[16], (N_ATT_LAYERS, N_Q_HEADS * HEAD_DIM, D_MODEL), (N_Q_HEADS * HEAD_DIM) ** -0.5),
        "mlp_w1": nrm(ks[17], (DEPTH, D_MODEL, D_FF), D_MODEL ** -0.5),
        "mlp_w2": nrm(ks[18], (DEPTH, D_FF, D_MODEL), D_FF ** -0.5),
    }


def reference(x, norm_g, rec_w_in, rec_conv_w, rec_conv_b, rg_w_r, rg_b_r, rg_w_i, rg_b_i,
              rg_lambda, hgrn_lb_logits, hgrn_norm_g, rec_w_out, att_w_qkv, att_sinks, att_w_o,
              mlp_w1, mlp_w2):
    s = x.shape[1]
    nb = s // QBLOCK
    pos = jnp.arange(s, dtype=jnp.float32)
    inv_freq = ROPE_THETA ** (-jnp.arange(0, HEAD_DIM, 2, dtype=jnp.float32) / HEAD_DIM)
    ang = pos[:, None] * inv_freq[None, :]
    cos = jnp.cos(ang)[:, None, :].astype(x.dtype)
    sin = jnp.sin(ang)[:, None, :].astype(x.dtype)
    qpos = jnp.arange(nb)[:, None, None] * QBLOCK + jnp.arange(QBLOCK)[None, :, None]
    kpos = (jnp.arange(nb)[:, None, None] - 1) * QBLOCK + jnp.arange(3 * QBLOCK)[None, None, :]
    band_mask = (jnp.abs(qpos - kpos) <= WINDOW) & (kpos >= 0) & (kpos < s)
    s_lb = jax.nn.softmax(hgrn_lb_logits.astype(jnp.float32), axis=1)
    lbs = jnp.cumsum(s_lb, axis=1) - s_lb[:, :1]

    h = x
    for layer in range(DEPTH):
        g = norm_g[layer]
        un = rmsnorm(h, g[0])
        if layer % 2 == 0:
            r = layer // 2
            m = rec_mixer(un, rec_w_in[r], rec_conv_w[r], rec_conv_b[r], rg_w_r[r], rg_b_r[r],
                          rg_w_i[r], rg_b_i[r], rg_lambda[r], lbs[:, r], hgrn_norm_g[r], rec_w_out[r])
        else:
            a = layer // 2
            m = window_attention(un, att_w_qkv[a], att_sinks[a], att_w_o[a], cos, sin, band_mask)
        h = h + rmsnorm(m.astype(h.dtype), g[1])
        m = sqrelu_mlp(rmsnorm(h, g[2]), mlp_w1[layer], mlp_w2[layer])
        h = h + rmsnorm(m.astype(h.dtype), g[3])
    return h
```

```python
import contextlib
import numpy as np
import concourse.bass as bass
import concourse.mybir as mybir
from concourse.bass_utils import run_bass_kernel_spmd

F32 = mybir.dt.float32
BF16 = mybir.dt.bfloat16
AF = mybir.ActivationFunctionType
ALU = mybir.AluOpType
AX = mybir.AxisListType

S_LEN = 2048
D = 1024
NCH = 8
TB = 512
NTB = 4
DFF = 4096
EPS = 1e-6
ENGS = ("pe", "act", "dve", "pool", "sp")


class Buf:
    __slots__ = ("name", "w", "r", "excl")

    def __init__(self, name="", excl=False):
        self.name = name
        self.w = None
        self.r = {}
        self.excl = excl


def I(name, **kw):
    return (name, kw)


class Sched:
    def __init__(self, nc):
        self.nc = nc
        self.sems = {}
        self.cnt = {}
        self.clockof = {}
        self.know = {e: {} for e in ENGS}
        self.prog = {e: [] for e in ENGS}
        self._ctx = []
        for e in ENGS:
            self._newsrc(e)
        self.nchan = 0

    def _newsrc(self, name):
        cm = self.nc.semaphore("s_" + name)
        sem = cm.__enter__()
        self._ctx.append(cm)
        self.sems[name] = sem
        self.cnt[name] = 0
        return sem

    def close(self):
        for cm in reversed(self._ctx):
            cm.__exit__(None, None, None)

    def _need(self, e, toks):
        k = self.know[e]
        waits = {}
        for t in toks:
            if t is None:
                continue
            s, c = t
            if s == e and e == "pe":
                continue
            if k.get(s, 0) >= c:
                continue
            if waits.get(s, 0) < c:
                waits[s] = c
        for s, c in waits.items():
            clk = self.clockof.get((s, c))
            if clk:
                for s2, c2 in clk.items():
                    if k.get(s2, 0) < c2:
                        k[s2] = c2
            if k.get(s, 0) < c:
                k[s] = c
        return list(waits.items())

    def _deps(self, reads, writes):
        toks = []
        for b in reads:
            toks.append(b.w)
            if b.excl:
                for s, c in b.r.items():
                    toks.append((s, c))
        for b in writes:
            toks.append(b.w)
            for s, c in b.r.items():
                toks.append((s, c))
        return toks

    muted = False

    def op(self, e, fn, reads=(), writes=(), inc=True):
        if self.muted:
            return None
        waits = self._need(e, self._deps(reads, writes))
        if inc:
            self.cnt[e] += 1
            tok = (e, self.cnt[e])
            self.clockof[tok] = dict(self.know[e])
        else:
            tok = (e, self.cnt[e] + 1)
        self.prog[e].append((waits, fn, e if inc else None, 1))
        for b in reads:
            if b.r.get(e, 0) < tok[1]:
                b.r[e] = tok[1]
        for b in writes:
            b.w = tok
            b.r = {}
        return tok

    def dma(self, q, fn, reads=(), writes=(), chan=None):
        if self.muted:
            return None
        if chan is None:
            chan = "c%d" % self.nchan
            self.nchan += 1
        if chan not in self.sems:
            self._newsrc(chan)
        waits = self._need(q, self._deps(reads, writes))
        self.cnt[chan] += 16
        tok = (chan, self.cnt[chan])
        self.clockof[tok] = dict(self.know[q])
        self.prog[q].append((waits, fn, chan, 16))
        for b in reads:
            if b.r.get(chan, 0) < tok[1]:
                b.r[chan] = tok[1]
        for b in writes:
            b.w = tok
            b.r = {}
        return tok

    def wait_all(self, e, toks):
        waits = self._need(e, toks)
        self.prog[e].append((waits, None, None, 0))

    def barrier(self, pe_waits=False):
        if self.muted:
            return
        toks = [(e, self.cnt[e]) for e in ("pe", "act", "dve", "pool") if self.cnt[e] > 0]
        toks += [(ch, c) for ch, c in self.cnt.items() if ch not in ENGS and not ch.startswith("wl") and c > 0]
        for e in ("pe", "act", "dve", "pool", "sp"):
            if e == "pe" and not pe_waits:
                continue
            self.wait_all(e, toks)

    def emit(self):
        nc = self.nc
        sems = self.sems
        prog = self.prog
        with nc.Block() as block:
            def run(name, engobj):
                for waits, fn, incsrc, amt in prog[name]:
                    for s, c in waits:
                        engobj.wait_ge(sems[s], c)
                    if fn is not None:
                        if isinstance(fn, tuple):
                            ins = getattr(engobj, fn[0])(**fn[1])
                        else:
                            ins = fn(engobj)
                        if incsrc is not None:
                            ins.then_inc(sems[incsrc], amt)

            @block.tensor
            def _(eng):
                run("pe", eng)

            @block.scalar
            def _(eng):
                run("act", eng)

            @block.vector
            def _(eng):
                run("dve", eng)

            @block.gpsimd
            def _(eng):
                run("pool", eng)

            @block.sync
            def _(eng):
                run("sp", eng)


def _par_layout():
    off = {}
    o = 0

    def add(name, n):
        nonlocal o
        off[name] = o
        o += n
    add("norm_g", 4 * 4 * 8)
    add("conv_w", 2 * 4 * 4)
    add("conv_b", 2 * 4)
    add("b_r", 2 * 2 * 4)
    add("b_i", 2 * 2 * 4)
    add("lam", 2 * 2 * 4)
    add("lbl", 2 * 2 * 4)
    add("hng", 2 * 4)
    add("sinks", 2 * 16)
    off["_n"] = o
    return off


PAR = _par_layout()


def _pack_params(norm_g, rec_conv_w, rec_conv_b, rg_b_r, rg_b_i, rg_lambda, hgrn_lb_logits,
                 hgrn_norm_g, att_sinks):
    par = np.zeros((128, PAR["_n"]), np.float32)

    def put(name, arr):
        a = np.asarray(arr, np.float32)
        lead = int(np.prod(a.shape[:-1]))
        n = a.shape[-1] // 128
        a = a.reshape(lead, n, 128).transpose(2, 0, 1).reshape(128, lead * n)
        par[:, PAR[name]:PAR[name] + lead * n] = a
    put("norm_g", norm_g)
    put("conv_w", rec_conv_w)
    put("conv_b", rec_conv_b)
    put("b_r", rg_b_r)
    put("b_i", rg_b_i)
    put("lam", rg_lambda)
    put("lbl", hgrn_lb_logits)
    put("hng", hgrn_norm_g)
    par[:, PAR["sinks"]:PAR["sinks"] + 32] = np.asarray(att_sinks, np.float32).reshape(1, 32)
    return par


def _cst_layout():
    off = {}
    o = 0

    def add(name, n):
        nonlocal o
        off[name] = o
        o += n
    add("ones", 128)
    add("ident", 128)
    add("rot", 128)
    add("e0", 128)
    add("e1", 128)
    add("mprev", 128)
    add("mnext", 128)
    add("hmf", 128)
    add("hmb", 128)
    add("cmask", 64)
    add("onesL", 128)
    add("onesR", 128)
    add("bdiag", 128)
    add("nmprev", 128)
    add("nmnext", 128)
    off["_n"] = o
    return off


CST = _cst_layout()


def _make_consts():
    c = np.zeros((128, CST["_n"]), np.float32)
    i = np.arange(128)
    c[:, CST["ones"]:CST["ones"] + 128] = 1.0
    c[i, CST["ident"] + i] = 1.0
    R = np.zeros((128, 128), np.float32)
    for m in range(128):
        hb = (m // 64) * 64
        r = m % 64
        if r < 32:
            R[hb + r + 32, m] = -1.0
        else:
            R[hb + r - 32, m] = 1.0
    c[:, CST["rot"]:CST["rot"] + 128] = R
    c[:64, CST["e0"]:CST["e0"] + 128] = 1.0
    c[64:, CST["e1"]:CST["e1"] + 128] = 1.0
    b = i[:, None]
    a = i[None, :]
    c[:, CST["mprev"]:CST["mprev"] + 128] = (a <= b)
    c[:, CST["mnext"]:CST["mnext"] + 128] = (b <= a)
    c[:, CST["nmprev"]:CST["nmprev"] + 128] = np.where(a <= b, 0.0, -30000.0)
    c[:, CST["nmnext"]:CST["nmnext"] + 128] = np.where(b <= a, 0.0, -30000.0)
    same = (b // 64) == (a // 64)
    c[:, CST["hmf"]:CST["hmf"] + 128] = same & (b <= a)
    c[:, CST["hmb"]:CST["hmb"] + 128] = same & (b >= a)
    c[:, CST["onesL"]:CST["onesL"] + 64] = 1.0
    c[:, CST["onesR"] + 64:CST["onesR"] + 128] = 1.0
    c[:64, CST["bdiag"]:CST["bdiag"] + 64] = 1.0
    c[64:, CST["bdiag"] + 64:CST["bdiag"] + 128] = 1.0
    cm = np.ones(64, np.float32)
    cm[0] = 0.0
    c[:, CST["cmask"]:CST["cmask"] + 64] = cm[None, :]
    return c


def _rope_tables():
    pos = np.arange(S_LEN, dtype=np.float32)
    inv_freq = (np.float32(10000.0) ** (-np.arange(0, 64, 2, dtype=np.float32) / np.float32(64))).astype(np.float32)
    ang = (pos[:, None] * inv_freq[None, :]).astype(np.float32)
    cos = np.cos(ang).astype(np.float32).T
    sin = np.sin(ang).astype(np.float32).T
    cosT = np.tile(cos, (4, 1))
    sinT = np.tile(sin, (4, 1))
    return np.ascontiguousarray(cosT), np.ascontiguousarray(sinT)


class Prog:
    NSLOT = 8

    def __init__(self, sublayers):
        self.sublayers = list(sublayers)
        nc = bass.Bass("TRN2", target_bir_lowering=False)
        self.nc = nc
        self.es = contextlib.ExitStack()
        dr = lambda name, shape, kind="ExternalInput": nc.dram_tensor(name, shape, F32, kind=kind).ap()
        self.d_x = dr("xT", [D, S_LEN])
        self.d_y = dr("yT", [D, S_LEN], "ExternalOutput")
        self.d_par = dr("par", [128, PAR["_n"]])
        self.d_cst = dr("cst", [128, CST["_n"]])
        self.d_cos = dr("cosT", [128, S_LEN])
        self.d_sin = dr("sinT", [128, S_LEN])
        self.d_yscr = nc.dram_tensor("yscr", [D, S_LEN], BF16, kind="Internal").ap()
        self.d_w = {
            "rec_w_in": dr("rec_w_in", [2, D, 3584]),
            "rg_w_r": dr("rg_w_r", [2, 2, 4, 128, 128]),
            "rg_w_i": dr("rg_w_i", [2, 2, 4, 128, 128]),
            "rec_w_out": dr("rec_w_out", [2, D, D]),
            "att_w_qkv": dr("att_w_qkv", [2, D, 1536]),
            "att_w_o": dr("att_w_o", [2, D, D]),
            "mlp_w1": dr("mlp_w1", [4, D, DFF]),
            "mlp_w2": dr("mlp_w2", [4, DFF, D]),
        }
        self.S = Sched(nc)
        self._alloc()
        self._build()
        self.S.emit()
        self.es.close()
        self.S.close()

    def sb(self, name, shape, dt=F32):
        return self.es.enter_context(self.nc.sbuf_tensor("sb_" + name, shape, dt))

    def _alloc(self):
        nc = self.nc
        self.hT = self.sb("hT", [128, NCH, S_LEN])
        self.HB = [[Buf("h%d_%d" % (c, t)) for t in range(NTB)] for c in range(NCH)]
        self.XT = self.sb("XT", [128, NCH, S_LEN], BF16)
        self.XB = [[Buf("x%d_%d" % (c, t)) for t in range(NTB)] for c in range(NCH)]
        self.wring = self.sb("wring", [128, self.NSLOT, 1024], BF16)
        self.WB = [Buf("w%d" % i) for i in range(self.NSLOT)]
        self.par = self.sb("par", [128, PAR["_n"]])
        self.parB = Buf("par")
        self.cst = self.sb("cst", [128, CST["_n"]], BF16)
        self.cstB = Buf("cst")
        self.der = self.sb("der", [128, 256])
        self.derB = Buf("der")
        self.ARENA_W = 22528
        self.AR = self.sb("arena", [128, self.ARENA_W])
        self.EB = self.ARENA_W - 7168
        self.MT = self.carve(self.EB, 4096).rearrange("p (c t) -> p c t", c=NCH)
        self.MB = [Buf("m%d" % c) for c in range(NCH)]
        self.SQ = self.carve(self.EB + 4096, 2048, BF16).rearrange("p (c t) -> p c t", c=NCH)
        self.SQB = [Buf("sq%d" % c) for c in range(NCH)]
        self.RS = self.carve(self.EB + 6144, 1024).rearrange("p (c t) -> p c t", c=2)
        self.RSB = [Buf("rs0"), Buf("rs1")]
        self.SQ2 = self.carve(12288, 2048, BF16).rearrange("p (c t) -> p c t", c=NCH)
        self.SQ2B = [Buf("sq2_%d" % c) for c in range(NCH)]
        self.PS = self.es.enter_context(nc.psum_tensor("ps", [128, 8, 512], F32))
        self.PB = [Buf("ps%d" % i, excl=True) for i in range(8)]
        self.stream_out = True
        self.handoff = True
        self.first_gemm_t_outer = False
        self.carry = None
        self.stored = set()
        self.out_toks = []
        self.wq_items = []
        self.wq_next_issue = 0
        self.wq_next_use = 0
        self.gcount = 0

    def carve(self, off_words, nwords, dt=F32):
        ap = self.AR[:, off_words:off_words + nwords]
        if dt == BF16:
            ap = ap.bitcast(BF16)
        return ap

    def pcol(self, name, idx):
        o = PAR[name] + idx
        return self.par[:, o:o + 1]

    def cmat(self, name, n=128):
        o = CST[name]
        return self.cst[:, o:o + n]

    def wq_plan(self, items):
        self.wq_items.extend(items)

    def _wq_issue(self, i):
        slot = i % self.NSLOT
        sl = self.wring[:, slot, :]
        for view_fn, src in self.wq_items[i]:
            dst = view_fn(sl)
            self.S.dma("pool", (lambda dst, src: lambda e: e.dma_start(out=dst, in_=src))(dst, src),
                       writes=[self.WB[slot]], chan="wl%d" % slot)

    def wq_get(self):
        i = self.wq_next_use
        self.wq_next_use += 1
        lim = min(i + self.NSLOT, len(self.wq_items))
        while self.wq_next_issue < lim:
            self._wq_issue(self.wq_next_issue)
            self.wq_next_issue += 1
        slot = i % self.NSLOT
        return self.wring[:, slot, :], self.WB[slot]

    def pieces_tb(self, W, KC, c0):
        items = []
        for k2 in range(KC // 2):
            src = W[k2 * 256:(k2 + 1) * 256, c0:c0 + 512].rearrange("(i p) c -> p i c", p=128)
            items.append([(lambda sl: sl.rearrange("p (i c) -> p i c", i=2), src)])
        return items

    def piece_cols(self, W, c0):
        src = W[:, c0:c0 + 128].rearrange("(k p) c -> p k c", p=128)
        return [[(lambda sl: sl.rearrange("p (k c) -> p k c", k=8), src)]]

    def gemm_tb(self, KC, rhs_fn, rhs_bufs_fn, evac, defer_evac=False):
        S = self.S
        pset = self.gcount % 2
        self.gcount += 1
        banks = [pset * 4 + i for i in range(4)]
        for k2 in range(KC // 2):
            sl, wb = self.wq_get()
            slv = sl.rearrange("p (i c) -> p i c", i=2)
            for i in range(2):
                kc = k2 * 2 + i
                for oc in range(4):
                    last = (i == 1 and oc == 3)
                    S.op("pe", (lambda b, l, r, st, sp: lambda e: e.matmul(self.PS[:, b, :], lhsT=l, rhs=r, start=st, stop=sp))(
                        banks[oc], slv[:, i, oc * 128:(oc + 1) * 128], rhs_fn(kc), kc == 0, kc == KC - 1),
                        reads=[wb] + rhs_bufs_fn(kc), writes=[self.PB[banks[oc]]], inc=last)
        def do_evac():
            for oc in range(4):
                evac(oc, self.PS[:, banks[oc], :], self.PB[banks[oc]])
        if defer_evac:
            return do_evac
        do_evac()

    def gemm_all(self, evac, defer_evac=False):
        S = self.S
        pset = self.gcount % 2
        self.gcount += 1
        banks = [pset * 4 + i for i in range(4)]
        sl, wb = self.wq_get()
        slv = sl.rearrange("p (k c) -> p k c", k=8)
        order = [(kc, t) for kc in range(8) for t in range(NTB)]
        if self.first_gemm_t_outer:
            order = [(kc, t) for t in range(NTB) for kc in range(8)]
            self.first_gemm_t_outer = False
        for idx, (kc, t) in enumerate(order):
            last = (idx == len(order) - 1)
            S.op("pe", I("matmul", out=self.PS[:, banks[t], :], lhsT=slv[:, kc, :], rhs=self.XT[:, kc, t * TB:(t + 1) * TB],
                         start=(kc == 0), stop=(kc == 7)),
                 reads=[wb, self.XB[kc][t]], writes=[self.PB[banks[t]]], inc=last)
        if defer_evac:
            return (lambda: evac(banks)), pset
        evac(banks)

    def rstd_from_sq(self, which, nchunks=NCH, dim=D, sq=None, sqb=None):
        S = self.S
        bank = 0 if (self.gcount % 2 == 0) else 4
        sq = self.SQ if sq is None else sq
        sqb = self.SQB if sqb is None else sqb
        for c in range(nchunks):
            S.op("pe", I("matmul", out=self.PS[:, bank, :], lhsT=self.cmat("ones"), rhs=sq[:, c, :], start=(c == 0), stop=(c == nchunks - 1)),
                 reads=[self.cstB, sqb[c]], writes=[self.PB[bank]], inc=(c == nchunks - 1))
        rs = self.RS[:, which, :]
        S.op("act", lambda e: e.activation(out=rs, in_=self.PS[:, bank, :], func=AF.Ln, bias=self.der[:, 0:1], scale=(self.der[:, 1:2] if dim == D else self.der[:, 2:3])),
             reads=[self.PB[bank], self.derB], writes=[self.RSB[which]])
        S.op("act", lambda e: e.activation(out=rs, in_=rs, func=AF.Exp, scale=-0.5),
             reads=[self.RSB[which]], writes=[self.RSB[which]])

    def norm_to_XT(self, tb, gidx):
        S = self.S
        ts = slice(tb * TB, (tb + 1) * TB)
        for c in range(NCH):
            S.op("act", I("activation", out=self.SQ2[:, c, :], in_=self.hT[:, c, ts], func=AF.Square),
                 reads=[self.HB[c][tb]], writes=[self.SQ2B[c]])
        self.rstd_from_sq(1, sq=self.SQ2, sqb=self.SQ2B)
        for c in range(NCH):
            S.op("dve", (lambda c: lambda e: e.scalar_tensor_tensor(
                out=self.XT[:, c, ts], in0=self.hT[:, c, ts], scalar=self.pcol("norm_g", gidx * 8 + c),
                in1=self.RS[:, 1, :], op0=ALU.mult, op1=ALU.mult))(c),
                reads=[self.HB[c][tb], self.RSB[1], self.parB], writes=[self.XB[c][tb]])

    def post_block(self, tb, gpost, gnext):
        self.post_block_a(tb, gpost)
        if gnext is not None:
            self.norm_to_XT(tb, gnext)
        elif self.stream_out:
            self.store_tb(tb)

    def store_tb(self, tb):
        yv = self.d_y.rearrange("(c p) t -> p c t", p=128)
        ts = slice(tb * TB, (tb + 1) * TB)
        for c in range(NCH):
            self.out_toks.append(self.S.dma("sp", I("dma_start", out=yv[:, c, ts], in_=self.hT[:, c, ts]),
                                            reads=[self.HB[c][tb]], chan="outc"))
        self.stored.add(tb)

    def post_block_a(self, tb, gpost):
        S = self.S
        ts = slice(tb * TB, (tb + 1) * TB)
        self.rstd_from_sq(0)
        for c in range(NCH):
            S.op("dve", (lambda c: lambda e: e.scalar_tensor_tensor(
                out=self.MT[:, c, :], in0=self.MT[:, c, :], scalar=self.pcol("norm_g", gpost * 8 + c),
                in1=self.RS[:, 0, :], op0=ALU.mult, op1=ALU.mult))(c),
                reads=[self.MB[c], self.RSB[0], self.parB], writes=[self.MB[c]])
            S.op("pool", I("tensor_tensor", out=self.hT[:, c, ts], in0=self.hT[:, c, ts], in1=self.MT[:, c, :], op=ALU.add),
                 reads=[self.MB[c], self.HB[c][tb]], writes=[self.HB[c][tb]])

    def evac_to_MT(self, c):
        S = self.S

        def f(oc, bank_ap, bank_buf):
            S.op("dve", lambda e: e.tensor_copy(out=self.MT[:, c, :], in_=bank_ap), reads=[bank_buf], writes=[self.MB[c]])
            S.op("act", lambda e: e.activation(out=self.SQ[:, c, :], in_=self.MT[:, c, :], func=AF.Square),
                 reads=[self.MB[c]], writes=[self.SQB[c]])
        return f

    def out_proj(self, layer, gnext):
        pend = None
        for tb in range(NTB):
            ts = slice(tb * TB, (tb + 1) * TB)
            evs = []
            for g in range(2):
                def evac2(oc, bank_ap, bank_buf, g=g):
                    self.evac_to_MT(g * 4 + oc)(oc, bank_ap, bank_buf)
                evs.append(self.gemm_tb(8, (lambda ts: lambda kc: self.XT[:, kc, ts])(ts), (lambda tb: lambda kc: [self.XB[kc][tb]])(tb),
                                        evac2, defer_evac=True))
                if pend is not None and g == 0:
                    self.post_block_a(pend, layer * 4 + 1)
            evs[0]()
            if pend is not None and gnext is not None:
                self.norm_to_XT(pend, gnext)
            evs[1]()
            pend = tb
        if self.handoff and gnext is not None:
            self.carry = (pend, layer * 4 + 1, gnext)
        else:
            self.post_block(pend, layer * 4 + 1, gnext)

    def plan_mlp(self, layer):
        w1 = self.d_w["mlp_w1"][layer]
        w2 = self.d_w["mlp_w2"][layer]
        items = []
        for tb in range(NTB):
            for g in range(8):
                items += self.pieces_tb(w1, 8, g * 512)
            for g in range(2):
                items += self.pieces_tb(w2, 32, g * 512)
        self.wq_plan(items)

    def mlp(self, layer, gnext):
        S = self.S
        HID = self.carve(0, 8192, BF16).rearrange("p (c t) -> p c t", c=32)
        HIDB = [Buf("hid%d" % c) for c in range(32)]
        REL = self.carve(8192, 4096).rearrange("p (s c t) -> p s c t", s=2, c=4)
        RELB = [[Buf() for _ in range(4)] for _ in range(2)]
        pend = None
        for tb in range(NTB):
            ts = slice(tb * TB, (tb + 1) * TB)
            for g in range(8):
                if self.carry is not None and g == 1:
                    self.post_block_a(self.carry[0], self.carry[1])
                if self.carry is not None and g == 3:
                    self.norm_to_XT(self.carry[0], self.carry[2])
                    self.carry = None
                if pend is not None and g == 1:
                    self.post_block_a(pend, layer * 4 + 3)
                if pend is not None and g == 3:
                    if gnext is not None:
                        self.norm_to_XT(pend, gnext)
                    elif self.stream_out:
                        self.store_tb(pend)
                    pend = None
                pset = self.gcount % 2

                def evac1(oc, bank_ap, bank_buf, g=g, pset=pset):
                    hc = g * 4 + oc
                    S.op("act", lambda e: e.activation(out=REL[:, pset, oc, :], in_=bank_ap, func=AF.Relu),
                         reads=[bank_buf], writes=[RELB[pset][oc]])
                    S.op("dve", lambda e: e.tensor_tensor(out=HID[:, hc, :], in0=REL[:, pset, oc, :], in1=REL[:, pset, oc, :], op=ALU.mult),
                         reads=[RELB[pset][oc]], writes=[HIDB[hc]])
                self.gemm_tb(8, lambda kc: self.XT[:, kc, ts], lambda kc: [self.XB[kc][tb]], evac1)
            for g in range(2):
                def evac2(oc, bank_ap, bank_buf, g=g):
                    self.evac_to_MT(g * 4 + oc)(oc, bank_ap, bank_buf)
                self.gemm_tb(32, lambda kc: HID[:, kc, :], lambda kc: [HIDB[kc]], evac2)
            pend = tb
        self.post_block(pend, layer * 4 + 3, gnext)

    def load_inputs(self):
        S = self.S
        S.dma("sp", lambda e: e.dma_start(out=self.par[:], in_=self.d_par[:, :]), writes=[self.parB])
        cstf = self.carve(0, CST["_n"])
        S.dma("sp", lambda e: e.dma_start(out=cstf, in_=self.d_cst[:, :]), writes=[self.cstB])
        S.op("dve", lambda e: e.tensor_copy(out=self.cst[:], in_=cstf), reads=[self.cstB], writes=[self.cstB])
        S.op("dve", lambda e: e.memset(self.der[:, 0:1], EPS), writes=[self.derB])
        S.op("dve", lambda e: e.memset(self.der[:, 1:2], 1.0 / D), writes=[self.derB])
        S.op("dve", lambda e: e.memset(self.der[:, 2:3], 1.0 / 128), writes=[self.derB])
        S.op("dve", lambda e: e.memset(self.der[:, 3:4], 0.125), writes=[self.derB])
        for col, val in ((4, 1.0), (5, -1.0), (6, 0.5), (7, -0.5), (8, 2.0), (9, 0.0), (10, -8.0)):
            S.op("dve", I("memset", ap=self.der[:, col:col + 1], constant=val), writes=[self.derB])
        xv = self.d_x.rearrange("(c p) t -> p c t", p=128)
        for c in range(NCH):
            for tb in range(NTB):
                ts = slice(tb * TB, (tb + 1) * TB)
                S.dma("sp", (lambda c, ts: lambda e: e.dma_start(out=self.hT[:, c, ts], in_=xv[:, c, ts]))(c, ts),
                      writes=[self.HB[c][tb]])

    def store_output(self):
        S = self.S
        S.muted = False
        for tb in range(NTB):
            if tb not in self.stored:
                self.store_tb(tb)
        S.wait_all("sp", [self.out_toks[-1]])

    def _build(self):
        subs = self.sublayers
        for s in subs:
            layer, kind = s // 2, s % 2
            if kind == 1:
                self.plan_mlp(layer)
            elif layer % 2 == 0:
                self.plan_rec(layer)
            else:
                self.plan_att(layer)
        STAGE = 9
        self.load_inputs()
        self.S.barrier()
        if STAGE == 0:
            self.store_output()
            return
        first = subs[0]
        self.first_gemm_t_outer = (first % 2 == 0)
        for tb in range(NTB):
            self.norm_to_XT(tb, (first // 2) * 4 + (2 if first % 2 else 0))
        self.S.barrier()
        if STAGE == 1:
            self.store_output()
            return
        for idx, s in enumerate(subs):
            layer, kind = s // 2, s % 2
            nxt = subs[idx + 1] if idx + 1 < len(subs) else None
            gnext = None if nxt is None else (nxt // 2) * 4 + (2 if nxt % 2 else 0)
            if kind == 1:
                self.mlp(layer, gnext)
            elif layer % 2 == 0:
                self.rec(layer, gnext)
            else:
                self.att(layer, gnext)
            if self.carry is None:
                if kind == 1 and nxt is not None:
                    self.S.barrier(pe_waits=False)
                    self.first_gemm_t_outer = True
                else:
                    self.S.barrier()
        self.store_output()

    def dcol(self, i):
        return self.der[:, i:i + 1]

    def plan_rec(self, layer):
        r = layer // 2
        W = self.d_w["rec_w_in"][r]
        items = []
        items += self.piece_cols(W, 0)
        for j in range(4):
            gates = []
            for d in range(2):
                for gi, nm in enumerate(("rg_w_r", "rg_w_i")):
                    m = d * 2 + gi
                    gates.append(((lambda m: lambda sl: sl[:, m * 128:(m + 1) * 128])(m), self.d_w[nm][r, d, j]))
            items.append(gates)
            items += self.piece_cols(W, 512 + j * 128)
            if j < 3:
                items += self.piece_cols(W, (j + 1) * 128)
        items += self.piece_cols(W, 2560)
        items += self.piece_cols(W, 1024)
        for j in range(4):
            items += self.piece_cols(W, 1536 + j * 128)
            items += self.piece_cols(W, 2048 + j * 128)
            if j < 3:
                items += self.piece_cols(W, 2560 + (j + 1) * 128)
                items += self.piece_cols(W, 1024 + (j + 1) * 128)
            items += self.piece_cols(W, 3072 + j * 128)
        Wo = self.d_w["rec_w_out"][r]
        for tb in range(NTB):
            for g in range(2):
                items += self.pieces_tb(Wo, 8, g * 512)
        self.wq_plan(items)

    def evac_copy_all(self, dst, dstB, eng="act"):
        S = self.S

        dl = dstB if isinstance(dstB, list) else [dstB]

        def f(banks):
            for t in range(NTB):
                b = banks[t]
                if eng == "act":
                    S.op("act", I("copy", out=dst[:, t * TB:(t + 1) * TB], in_=self.PS[:, b, :]), reads=[self.PB[b]], writes=dl)
                else:
                    S.op("dve", I("tensor_copy", out=dst[:, t * TB:(t + 1) * TB], in_=self.PS[:, b, :]), reads=[self.PB[b]], writes=dl)
        return f

    def evac_act_all(self, dst, dstB, func, bias=None, scale=None):
        S = self.S
        dl = dstB if isinstance(dstB, list) else [dstB]

        def f(banks):
            for t in range(NTB):
                b = banks[t]
                kw = {}
                if bias is not None:
                    kw["bias"] = bias
                    kw["scale"] = scale
                S.op("act", I("activation", out=dst[:, t * TB:(t + 1) * TB], in_=self.PS[:, b, :], func=func, **kw),
                     reads=[self.PB[b], self.derB, self.parB], writes=dl)
        return f

    def rec(self, layer, gnext):
        S = self.S
        PS = self.PS
        PB = self.PB
        r = layer // 2
        one, mone, zero = self.dcol(4), self.dcol(5), self.dcol(9)
        FT = lambda off: self.carve(off, 2048)
        lam = self.par[:, PAR["lam"] + r * 8:PAR["lam"] + r * 8 + 8]
        KD = self.der[:, 16:24]
        S.op("act", I("activation", out=KD, in_=lam, func=AF.Exp, bias=zero, scale=mone), reads=[self.parB, self.derB], writes=[self.derB])
        S.op("act", I("activation", out=KD, in_=KD, func=AF.Ln, bias=one, scale=one), reads=[self.derB], writes=[self.derB])
        S.op("dve", I("tensor_scalar", out=KD, in0=KD, scalar1=-8.0, scalar2=None, op0=ALU.mult), reads=[self.derB], writes=[self.derB])
        LB = self.der[:, 24:32]
        LNOML = self.der[:, 32:40]
        TMP = self.der[:, 40:48]
        TMP2 = self.der[:, 48:56]
        for d in range(2):
            l0 = self.par[:, PAR["lbl"] + (d * 2 + 0) * 4:PAR["lbl"] + (d * 2 + 0) * 4 + 4]
            l1 = self.par[:, PAR["lbl"] + (d * 2 + 1) * 4:PAR["lbl"] + (d * 2 + 1) * 4 + 4]
            t = TMP[:, d * 4:d * 4 + 4]
            t2 = TMP2[:, d * 4:d * 4 + 4]
            lb = LB[:, d * 4:d * 4 + 4]
            S.op("dve", I("tensor_tensor", out=t, in0=l1, in1=l0, op=ALU.subtract), reads=[self.parB, self.derB], writes=[self.derB])
            S.op("act", I("activation", out=t2, in_=t, func=AF.Exp, bias=zero, scale=mone), reads=[self.derB], writes=[self.derB])
            S.op("act", I("activation", out=t, in_=t, func=AF.Exp), reads=[self.derB], writes=[self.derB])
            S.op("dve", I("tensor_scalar", out=t, in0=t, scalar1=1.0, scalar2=None, op0=ALU.add), reads=[self.derB], writes=[self.derB])
            S.op("dve", I("reciprocal", out=t, in_=t), reads=[self.derB], writes=[self.derB])
            S.op("dve", I("tensor_scalar", out=t2, in0=t2, scalar1=1.0, scalar2=None, op0=ALU.add), reads=[self.derB], writes=[self.derB])
            S.op("dve", I("reciprocal", out=t2, in_=t2), reads=[self.derB], writes=[self.derB])
            if r == 0:
                S.op("dve", I("tensor_tensor", out=lb, in0=t, in1=t, op=ALU.subtract), reads=[self.derB], writes=[self.derB])
            else:
                S.op("dve", I("tensor_tensor", out=lb, in0=t, in1=t2, op=ALU.add), reads=[self.derB], writes=[self.derB])
                S.op("dve", I("tensor_tensor", out=lb, in0=lb, in1=t, op=ALU.subtract), reads=[self.derB], writes=[self.derB])
        S.op("act", I("activation", out=LNOML, in_=LB, func=AF.Ln, bias=one, scale=mone), reads=[self.derB], writes=[self.derB])

        P_ = self.carve(0, 2052)
        PB_ = Buf()
        C_ = FT(2052)
        CB_ = Buf()
        Q_ = FT(4100)
        QB_ = Buf()
        R_ = FT(6148)
        RB_ = Buf()
        H_ = FT(8196)
        HB_ = Buf()
        Y_ = FT(10244)
        YB_ = Buf()
        XCB = self.carve(12292, 1024, BF16)
        XCBB = Buf()
        YO = self.carve(13316, 1024, BF16)
        YOB = Buf()
        T1 = P_[:, 0:2048]
        T1b = FT(14340)
        Qb = FT(16388)
        Rb = FT(18436)
        PB2_ = Buf()
        PBL = [PB_, PB2_]
        T1Bp = [[PB_, PB2_], [Buf(), Buf()]]
        QBp = [[Buf(), Buf()], [Buf(), Buf()]]
        RBp = [[Buf(), Buf()], [Buf(), Buf()]]
        YBp = [Buf(), Buf()]
        HBp = [Buf(), Buf()]
        yv = self.d_yscr.rearrange("(c p) t -> p c t", p=128)
        S.op("dve", I("memset", ap=P_[:, 0:2], constant=0.0), writes=PBL)
        S.op("dve", I("memset", ap=P_[:, 2050:2052], constant=0.0), writes=PBL)
        for j in range(4):
            cw = lambda k: self.pcol("conv_w", (r * 4 + k) * 4 + j)
            cb = self.pcol("conv_b", r * 4 + j)
            if j > 0:
                S.op("dve", I("memset", ap=P_[:, 0:2], constant=0.0), writes=PBL)
            if j == 0:
                self.gemm_all(self.evac_copy_all(P_[:, 2:2050], PBL))
            else:
                xa_ev()
            S.op("act", I("activation", out=C_, in_=P_[:, 0:2048], func=AF.Identity, bias=cb, scale=cw(0)), reads=PBL + [self.parB], writes=[CB_])
            for k in range(1, 4):
                S.op("dve", I("scalar_tensor_tensor", out=C_, in0=P_[:, k:k + 2048], scalar=cw(k), in1=C_, op0=ALU.mult, op1=ALU.add),
                     reads=PBL + [CB_, self.parB], writes=[CB_])
            S.op("act", I("copy", out=XCB, in_=C_), reads=[CB_], writes=[XCBB])
            gsl, gwb = self.wq_get()

            def chain(d):
                kd = self.dcol(16 + d * 4 + j)
                NP_ = 2
                L = 2048 // NP_
                T1d, T1B = (T1, T1Bp[0]) if d == 0 else (T1b, T1Bp[1])
                Qd, QdB = (Q_, QBp[0]) if d == 0 else (Qb, QBp[1])
                Rd, RdB = (R_, RBp[0]) if d == 0 else (Rb, RBp[1])
                Td, TdB = (Y_, YBp) if d == 0 else (H_, HBp)
                for gi, (dst, dstB, bname) in enumerate(((T1d, T1B, "b_r"), (Qd, QdB, "b_i"))):
                    m = d * 2 + gi
                    pset = self.gcount % 2
                    self.gcount += 1
                    banks = [pset * 4 + i for i in range(4)]
                    bcol = self.pcol(bname, (r * 2 + d) * 4 + j)
                    for t in range(NTB):
                        S.op("pe", I("matmul", out=PS[:, banks[t], :], lhsT=gsl[:, m * 128:(m + 1) * 128], rhs=XCB[:, t * TB:(t + 1) * TB],
                                     start=True, stop=True), reads=[gwb, XCBB], writes=[PB[banks[t]]])
                    for t in range(NTB):
                        S.op("act", I("activation", out=dst[:, t * TB:(t + 1) * TB], in_=PS[:, banks[t], :], func=AF.Sigmoid, bias=bcol, scale=one),
                             reads=[PB[banks[t]], self.derB, self.parB], writes=[dstB[t * TB // L]])
                    yield
                order = range(NP_) if d == 0 else range(NP_ - 1, -1, -1)
                for p in order:
                    c_ = slice(p * L, (p + 1) * L)
                    S.op("act", I("activation", out=Rd[:, c_], in_=T1d[:, c_], func=AF.Tanh, bias=zero, scale=kd), reads=[T1B[p], self.derB], writes=[RdB[p]])
                yield
                for p in order:
                    c_ = slice(p * L, (p + 1) * L)
                    S.op("act", I("activation", out=T1d[:, c_], in_=T1d[:, c_], func=AF.Exp, bias=zero, scale=kd), reads=[T1B[p], self.derB], writes=[T1B[p]])
                yield
                for p in order:
                    c_ = slice(p * L, (p + 1) * L)
                    S.op("dve", I("tensor_tensor", out=Td[:, c_], in0=T1d[:, c_], in1=T1d[:, c_], op=ALU.mult), reads=[T1B[p]], writes=[TdB[p]])
                yield
                for p in order:
                    c_ = slice(p * L, (p + 1) * L)
                    S.op("dve", I("scalar_tensor_tensor", out=Rd[:, c_], in0=Td[:, c_], scalar=1.0, in1=Rd[:, c_], op0=ALU.add, op1=ALU.mult),
                         reads=[TdB[p], RdB[p]], writes=[RdB[p]])
                yield
                for p in order:
                    c_ = slice(p * L, (p + 1) * L)
                    S.op("act", I("activation", out=Rd[:, c_], in_=Rd[:, c_], func=AF.Sqrt, bias=zero, scale=mone), reads=[RdB[p], self.derB], writes=[RdB[p]])
                yield
                for p in order:
                    c_ = slice(p * L, (p + 1) * L)
                    S.op("dve", I("tensor_tensor", out=Qd[:, c_], in0=Qd[:, c_], in1=C_[:, c_], op=ALU.mult), reads=[QdB[p], CB_], writes=[QdB[p]])
                yield
                for p in order:
                    c_ = slice(p * L, (p + 1) * L)
                    S.op("dve", I("tensor_tensor", out=Qd[:, c_], in0=Qd[:, c_], in1=Rd[:, c_], op=ALU.mult), reads=[QdB[p], RdB[p]], writes=[QdB[p]])
                yield
                for p in order:
                    c_ = slice(p * L, (p + 1) * L)
                    if d == 0:
                        init = 0.0 if p == 0 else Y_[:, p * L - 1:p * L]
                        S.op("dve", I("tensor_tensor_scan", out=Y_[:, c_], data0=T1d[:, c_], data1=Qd[:, c_], initial=init, op0=ALU.mult, op1=ALU.add),
                             reads=[T1B[p], QdB[p]] + ([YBp[p - 1]] if p > 0 else []), writes=[YBp[p]])
                    else:
                        init = 0.0 if p == NP_ - 1 else H_[:, (p + 1) * L:(p + 1) * L + 1]
                        S.op("dve", I("tensor_tensor_scan", out=H_[:, c_][:, ::-1], data0=T1d[:, c_][:, ::-1], data1=Qd[:, c_][:, ::-1], initial=init,
                                      op0=ALU.mult, op1=ALU.add),
                             reads=[T1B[p], QdB[p]] + ([HBp[p + 1]] if p < NP_ - 1 else []), writes=[HBp[p]])
                yield
            gens = [chain(0), chain(1)]
            for _ in range(2):
                for gobj in gens:
                    next(gobj)
            ga_ev, _ = self.gemm_all(self.evac_act_all(R_, RBp[0], AF.Gelu), defer_evac=True)
            if j < 3:
                xa_ev, _ = self.gemm_all(self.evac_copy_all(P_[:, 2:2050], PBL), defer_evac=True)
            while gens:
                for gobj in list(gens):
                    try:
                        next(gobj)
                    except StopIteration:
                        gens.remove(gobj)
            S.op("dve", I("tensor_tensor", out=Y_, in0=Y_, in1=H_, op=ALU.add), reads=YBp + HBp, writes=YBp)
            ga_ev()
            S.op("dve", I("tensor_tensor", out=YO, in0=Y_, in1=R_, op=ALU.mult), reads=YBp + RBp[0], writes=[YOB] + RBp[0])
            S.dma("sp", I("dma_start", out=yv[:, j, :], in_=YO), reads=[YOB], chan="ysc")
        S.barrier()

        QF = FT(0)
        QFB = Buf()
        Z = FT(2048)
        ZB = [Buf(), Buf()]
        A = FT(4096)
        AB = [Buf(), Buf()]
        BT = FT(6144)
        BTB = [Buf(), Buf()]
        OT = FT(8192)
        OTB = [Buf() for _ in range(16)]
        QE = [self.carve(10240 + i * 1024, 1024, BF16) for i in range(2)]
        QEB = [Buf(), Buf()]
        KE = [self.carve(12288 + i * 1024, 1024, BF16) for i in range(2)]
        KEB = [Buf(), Buf()]
        KDT = self.carve(14336, 1024, BF16)
        KDTB = [Buf(), Buf()]
        KDTOK = [self.carve(15360 + i * 1024, 1024, BF16).rearrange("p (t d) -> p t d", t=16) for i in range(2)]
        KDTOKB = [Buf(), Buf()]
        VTOK = self.carve(17408, 1024, BF16).rearrange("p (t d) -> p t d", t=16)
        VTOKB = Buf()
        SBF = self.carve(18432, 2112, BF16).rearrange("p (n v) -> p n v", n=33)
        SBFB = [Buf() for _ in range(33)]
        SF = self.carve(20544, 256).rearrange("p (b v) -> p b v", b=2)
        SFB = [Buf(), Buf()]
        ATTB = self.carve(20800, 128, BF16).rearrange("p (b c) -> p b c", b=2)
        ATTBB = [Buf(), Buf()]
        MASKC = self.carve(20928, 1024, BF16)
        MASKB = Buf()
        DEC = [self.carve(21952 + i * 32, 32) for i in range(2)]
        EMID = [self.carve(22016 + i * 32, 32) for i in range(2)]
        DECB = [Buf(), Buf()]
        BL = self.carve(22080, 32)
        BM = self.carve(22112, 32)
        BLB = [Buf(), Buf()]
        SBFd = [SBF, self.carve(2048, 2112, BF16).rearrange("p (n v) -> p n v", n=33)]
        SBFBd = [SBFB, [Buf() for _ in range(33)]]
        SFd = [SF, self.carve(4224, 256).rearrange("p (b v) -> p b v", b=2)]
        SFBd = [SFB, [Buf(), Buf()]]
        ATTBd = [ATTB, self.carve(4480, 128, BF16).rearrange("p (b c) -> p b c", b=2)]
        ATTBBd = [ATTBB, [Buf(), Buf()]]
        OT2 = BT
        OUTd = [OT, OT2]
        OUTBd = [OTB, [Buf() for _ in range(16)]]
        G = Z
        RSH = A[:, 0:512]
        SQh2 = [self.carve(6144 + i * 256, 256, BF16) for i in range(2)]
        SQhB2 = [Buf(), Buf()]
        RSH2 = [A[:, i * 512:(i + 1) * 512] for i in range(2)]
        RSHB2 = [Buf(), Buf()]
        S.op("dve", I("tensor_copy", out=MASKC.rearrange("p (n c) -> p n c", c=64),
                      in_=self.cst[:, CST["cmask"]:CST["cmask"] + 64].rearrange("p (o c) -> p o c", o=1).to_broadcast([128, 32, 64])),
             reads=[self.cstB], writes=[MASKB])
        S.op("dve", I("memset", ap=SBF[:, 0, :], constant=0.0), writes=[SBFB[0]])
        ident = self.cmat("ident")
        ch = lambda ap: ap.rearrange("p (n c) -> p n c", c=64)
        bc = lambda ap: ap.rearrange("p (n o) -> p n o", o=1).to_broadcast([128, 32, 64])

        def prep(j, d, zev, zset):
            lbc = self.dcol(24 + d * 4 + j)
            lno = self.dcol(32 + d * 4 + j)
            rv = (lambda ap: ap) if d == 0 else (lambda ap: ap[:, ::-1])
            pmid = 31 if d == 0 else 32
            plast = 63 if d == 0 else 0
            NP_ = 2
            L = 2048 // NP_
            NC_ = 32 // NP_
            cs = lambda p: slice(p * L, (p + 1) * L)
            ns = lambda p: slice(p * NC_, (p + 1) * NC_)
            bcp = lambda ap: ap.rearrange("p (n o) -> p n o", o=1).to_broadcast([128, NC_, 64])
            zev()
            yield
            for p in range(NP_):
                S.op("act", I("activation", out=A[:, cs(p)], in_=Z[:, cs(p)], func=AF.Exp, bias=zero, scale=mone), reads=[ZB[p], self.derB], writes=[AB[p]])
            yield
            for p in range(NP_):
                S.op("act", I("activation", out=BT[:, cs(p)], in_=A[:, cs(p)], func=AF.Ln, bias=one, scale=one), reads=[AB[p], self.derB], writes=[BTB[p]])
            yield
            for p in range(NP_):
                S.op("act", I("activation", out=A[:, cs(p)], in_=A[:, cs(p)], func=AF.Ln, bias=one, scale=lbc), reads=[AB[p], self.derB], writes=[AB[p]])
            yield
            for p in range(NP_):
                S.op("dve", I("tensor_tensor", out=A[:, cs(p)], in0=A[:, cs(p)], in1=BT[:, cs(p)], op=ALU.subtract), reads=[AB[p], BTB[p]], writes=[AB[p]])
            yield
            for p in range(NP_):
                S.op("dve", I("scalar_tensor_tensor", out=Z[:, cs(p)], in0=Z[:, cs(p)], scalar=-1.0, in1=BT[:, cs(p)], op0=ALU.mult, op1=ALU.subtract),
                     reads=[ZB[p], BTB[p]], writes=[ZB[p]])
            yield
            for p in range(NP_):
                S.op("dve", I("tensor_tensor_scan", out=rv(BT[:, cs(p)]), data0=MASKC[:, 0:L], data1=rv(A[:, cs(p)]), initial=0.0,
                              op0=ALU.mult, op1=ALU.add), reads=[MASKB, AB[p]], writes=[BTB[p]])
            yield
            for p in range(NP_):
                S.op("act", I("copy", out=BL[:, ns(p)], in_=ch(BT[:, cs(p)])[:, :, plast]), reads=[BTB[p]], writes=[BLB[p]])
                S.op("act", I("copy", out=BM[:, ns(p)], in_=ch(BT[:, cs(p)])[:, :, pmid]), reads=[BTB[p]], writes=[BLB[p]])
                S.op("act", I("activation", out=DEC[d][:, ns(p)], in_=BL[:, ns(p)], func=AF.Exp), reads=[BLB[p]], writes=[DECB[d]])
                S.op("act", I("activation", out=EMID[d][:, ns(p)], in_=BM[:, ns(p)], func=AF.Exp), reads=[BLB[p]], writes=[DECB[d]])
            yield
            for p in range(NP_):
                S.op("dve", I("tensor_tensor", out=ch(A[:, cs(p)]), in0=ch(BT[:, cs(p)]), in1=bcp(BM[:, ns(p)]), op=ALU.subtract),
                     reads=[BTB[p], BLB[p], AB[p]], writes=[AB[p]])
            yield
            for p in range(NP_):
                S.op("dve", I("tensor_tensor", out=ch(BT[:, cs(p)]), in0=bcp(BL[:, ns(p)]), in1=ch(BT[:, cs(p)]), op=ALU.subtract),
                     reads=[BTB[p], BLB[p]], writes=[BTB[p]])
            yield
            for p in range(NP_):
                S.op("dve", I("tensor_tensor", out=BT[:, cs(p)], in0=BT[:, cs(p)], in1=Z[:, cs(p)], op=ALU.add), reads=[BTB[p], ZB[p]], writes=[BTB[p]])
            yield
            for p in range(NP_):
                S.op("act", I("activation", out=KDT[:, cs(p)], in_=BT[:, cs(p)], func=AF.Exp, bias=lno, scale=one), reads=[BTB[p], self.derB], writes=[KDTB[p]])
            yield
            for p in range(NP_):
                S.op("dve", I("tensor_tensor", out=BT[:, cs(p)], in0=Z[:, cs(p)], in1=A[:, cs(p)], op=ALU.subtract), reads=[ZB[p], AB[p], BTB[p]], writes=[BTB[p]])
            yield
            for p in range(NP_):
                S.op("act", I("activation", out=KE[d][:, cs(p)], in_=BT[:, cs(p)], func=AF.Exp, bias=lno, scale=one), reads=[BTB[p], self.derB], writes=[KEB[d]])
            yield
            for p in range(NP_):
                S.op("act", I("activation", out=A[:, cs(p)], in_=A[:, cs(p)], func=AF.Exp), reads=[AB[p]], writes=[AB[p]])
            yield
            for p in range(NP_):
                S.op("dve", I("tensor_tensor", out=QE[d][:, cs(p)], in0=A[:, cs(p)], in1=QF[:, cs(p)], op=ALU.mult), reads=[AB[p], QFB], writes=[QEB[d]])
            yield
            for q4 in range(4):
                b = 4 * zset + q4
                pb16 = PS[:, b, :].bitcast(BF16)
                for i in range(4):
                    tt = q4 * 4 + i
                    S.op("pe", I("transpose", out=pb16[:, i * 128:(i + 1) * 128], in_=KDT[:, tt * 128:(tt + 1) * 128], identity=ident),
                         reads=[KDTB[tt * 128 // L], self.cstB], writes=[PB[b]], inc=(i == 3))
                S.op("act", I("copy", out=KDTOK[d][:, q4 * 4:(q4 + 1) * 4, :], in_=pb16[:, 0:512].rearrange("p (t d) -> p t d", t=4)),
                     reads=[PB[b]], writes=[KDTOKB[d]])
                yield

        def passes(j, d):
            hm = self.cmat("hmf" if d == 0 else "hmb")
            SBF_, SBFB_, SF_, SFB_, ATT_, ATTB_, OUT_, OUTB_ = SBFd[d], SBFBd[d], SFd[d], SFBd[d], ATTBd[d], ATTBBd[d], OUTd[d], OUTBd[d]
            bb = 4 * d
            S.op("dve", I("memset", ap=SF_[:, 0, :], constant=0.0), writes=[SFB_[0]])
            S.op("dve", I("memset", ap=SBF_[:, 0, :], constant=0.0), writes=[SBFB_[0]])

            def tile_out(k):
                tt = k if d == 0 else 15 - k
                ab = k % 2
                b = bb + 2 + (k % 2)
                tsl = slice(tt * 128, (tt + 1) * 128)
                S.op("pe", I("matmul", out=PS[:, b, 0:128], lhsT=KE[d][:, tsl], rhs=QE[d][:, tsl], start=True, stop=True),
                     reads=[KEB[d], QEB[d]], writes=[PB[b]])
                S.op("dve", I("tensor_tensor", out=ATT_[:, ab, :], in0=PS[:, b, 0:128], in1=hm, op=ALU.mult),
                     reads=[PB[b], self.cstB], writes=[ATTB_[ab]])
                S.op("pe", I("matmul", out=PS[:, b, 128:256], lhsT=VTOK[:, tt, :], rhs=ATT_[:, ab, :], start=True, stop=False,
                             skip_group_check=True),
                     reads=[VTOKB, ATTB_[ab]], writes=[PB[b]], inc=False)
                cis = (2 * tt, 2 * tt + 1)
                for ci in cis:
                    n = ci if d == 0 else 31 - ci
                    S.op("pe", I("matmul", out=PS[:, b, 128 + (ci % 2) * 64:128 + (ci % 2) * 64 + 64], lhsT=SBF_[:, n, :],
                                 rhs=QE[d][:, ci * 64:ci * 64 + 64], start=False, stop=(ci == cis[-1]), skip_group_check=True),
                         reads=[SBFB_[n], QEB[d]], writes=[PB[b]], inc=(ci == cis[-1]))
                S.op("act", I("copy", out=OUT_[:, tsl], in_=PS[:, b, 128:256]), reads=[PB[b]], writes=[OUTB_[tt]])
            for n in range(31):
                cn = n if d == 0 else 31 - n
                tt, pr = cn // 2, slice((cn % 2) * 64, (cn % 2) * 64 + 64)
                b = bb + (n % 2)
                col = 0
                S.op("pe", I("matmul", out=PS[:, b, col:col + 128], lhsT=KDTOK[d][pr, tt, :], rhs=VTOK[pr, tt, :], start=True, stop=True),
                     reads=[KDTOKB[d], VTOKB], writes=[PB[b]])
                cur, nxt = n % 2, (n + 1) % 2
                S.op("dve", I("scalar_tensor_tensor", out=SF_[:, nxt, :], in0=SF_[:, cur, :], scalar=DEC[d][:, cn:cn + 1],
                              in1=PS[:, b, col:col + 128], op0=ALU.mult, op1=ALU.add),
                     reads=[SFB_[cur], DECB[d], PB[b]], writes=[SFB_[nxt]])
                cnn = (n + 1) if d == 0 else 31 - (n + 1)
                S.op("act", I("activation", out=SBF_[:, n + 1, :], in_=SF_[:, nxt, :], func=AF.Identity, bias=zero, scale=EMID[d][:, cnn:cnn + 1]),
                     reads=[SFB_[nxt], DECB[d], self.derB], writes=[SBFB_[n + 1]])
                yield
                if n % 2 == 0:
                    tile_out(n // 2)
                    yield

        def run_interleaved(gens_w):
            gens_w = [[g, w] for g, w in gens_w]
            while gens_w:
                for item in list(gens_w):
                    g, w = item
                    for _ in range(w):
                        try:
                            next(g)
                        except StopIteration:
                            gens_w.remove(item)
                            break

        def vq_gemms():
            pset = self.gcount % 2
            self.gcount += 1
            banks = [pset * 4 + i for i in range(4)]
            sl, wb = self.wq_get()
            slv = sl.rearrange("p (k c) -> p k c", k=8)
            for tt in range(16):
                b = banks[tt // 4]
                col = (tt % 4) * 128
                for kc in range(8):
                    S.op("pe", I("matmul", out=PS[:, b, col:col + 128], lhsT=self.XT[:, kc, tt * 128:(tt + 1) * 128],
                                 rhs=slv[:, kc, :], start=(kc == 0), stop=(kc == 7)),
                         reads=[wb, self.XB[kc][tt // 4]], writes=[PB[b]], inc=(kc == 7 and tt % 4 == 3))
            for bi in range(4):
                S.op("act", I("copy", out=VTOK[:, bi * 4:(bi + 1) * 4, :], in_=PS[:, banks[bi], :].rearrange("p (t d) -> p t d", t=4)),
                     reads=[PB[banks[bi]]], writes=[VTOKB])
            self.gemm_all(self.evac_copy_all(QF, QFB, "act"))

        vq_gemms()
        for j in range(4):
            zev0, zset0 = self.gemm_all(self.evac_copy_all(Z, ZB, "dve"), defer_evac=True)
            zev1, zset1 = self.gemm_all(self.evac_copy_all(Z, ZB, "dve"), defer_evac=True)
            run_interleaved([(prep(j, 0, zev0, zset0), 1)])
            run_interleaved([(prep(j, 1, zev1, zset1), 1)])
            S.barrier()
            run_interleaved([(passes(j, 0), 1), (passes(j, 1), 1)])
            S.barrier()
            for t in range(NTB):
                ts = slice(t * TB, (t + 1) * TB)
                S.op("dve", I("tensor_tensor", out=OT[:, ts], in0=OT[:, ts], in1=OT2[:, ts], op=ALU.add),
                     reads=OTB[t * 4:(t + 1) * 4] + OUTBd[1][t * 4:(t + 1) * 4], writes=OTB[t * 4:(t + 1) * 4])
            if j < 3:
                vq_gemms()
            self.gemm_all(self.evac_act_all(G, ZB, AF.Silu))
            hcol = self.pcol("hng", r * 4 + j)
            for t in range(NTB):
                ts = slice(t * TB, (t + 1) * TB)
                k2 = t % 2
                sqh = SQh2[k2]
                rsh = RSH2[k2]
                S.op("act", I("activation", out=sqh, in_=OT[:, ts], func=AF.Square), reads=OTB[t * 4:(t + 1) * 4] + BTB, writes=[SQhB2[k2]] + BTB)
                b = (self.gcount % 2) * 4 + k2
                S.op("pe", I("matmul", out=PS[:, b, :], lhsT=self.cmat("ones"), rhs=sqh, start=True, stop=True),
                     reads=[self.cstB, SQhB2[k2]], writes=[PB[b]])
                S.op("act", I("activation", out=rsh, in_=PS[:, b, :], func=AF.Ln, bias=self.dcol(0), scale=self.dcol(2)),
                     reads=[PB[b], self.derB] + AB, writes=[RSHB2[k2]] + AB)
                S.op("act", I("activation", out=rsh, in_=rsh, func=AF.Exp, scale=-0.5), reads=[RSHB2[k2]], writes=[RSHB2[k2]])
                S.op("dve", I("scalar_tensor_tensor", out=OT[:, ts], in0=OT[:, ts], scalar=hcol, in1=rsh, op0=ALU.mult, op1=ALU.mult),
                     reads=OTB[t * 4:(t + 1) * 4] + [RSHB2[k2], self.parB], writes=OTB[t * 4:(t + 1) * 4])
                S.op("dve", I("tensor_tensor", out=QE[0][:, ts], in0=OT[:, ts], in1=G[:, ts], op=ALU.mult),
                     reads=OTB[t * 4:(t + 1) * 4] + ZB + [QEB[0]], writes=[QEB[0]])
            self.gcount += 1
            S.dma("sp", I("dma_start", out=yv[:, 4 + j, :], in_=QE[0]), reads=[QEB[0]], chan="ysc")
        S.barrier()
        for c in range(NCH):
            S.dma("sp", I("dma_start", out=self.XT[:, c, :], in_=yv[:, c, :]), reads=[QEB[0], YOB], writes=self.XB[c], chan="yld%d" % c)
        r_ = r
        self.out_proj(layer, gnext)

    def plan_att(self, layer):
        a = layer // 2
        W = self.d_w["att_w_qkv"][a]
        items = []
        for oc in range(8):
            items += self.piece_cols(W, oc * 128)
        for g in range(4):
            src = W[:, 1024 + g * 64:1024 + (g + 1) * 64].rearrange("(k p) c -> p k c", p=128)
            items.append([(lambda sl: sl.rearrange("p (k c) -> p k c", k=8)[:, :, 0:64], src),
                          (lambda sl: sl.rearrange("p (k c) -> p k c", k=8)[:, :, 64:128], src)])
        for vh in range(2):
            items += self.piece_cols(W, 1280 + vh * 128)
        Wo = self.d_w["att_w_o"][a]
        for tb in range(NTB):
            for g in range(2):
                items += self.pieces_tb(Wo, 8, g * 512)
        self.wq_plan(items)

    def att(self, layer, gnext):
        S = self.S
        PS = self.PS
        PB = self.PB
        a = layer // 2
        QR = self.carve(0, 12288, BF16).rearrange("p (c t) -> p c t", c=12)
        QRB = [[Buf() for _ in range(NTB)] for _ in range(12)]
        VD = self.carve(12288, 4096, BF16).rearrange("p (t g d) -> p t g d", t=16, g=4)
        VDB = [Buf() for _ in range(16)]
        CS = self.carve(16384, 4096).rearrange("p (s t) -> p s t", s=2)
        CSB = Buf()
        QFs = [self.carve(12288 + i * 512, 512) for i in range(4)]
        QF2s = [self.carve(14336 + i * 512, 512) for i in range(4)]
        QBs = [self.carve(20480 + i * 256, 256, BF16) for i in range(4)]
        QFB = [Buf() for _ in range(4)]
        QF2B = [Buf() for _ in range(4)]
        QBB = [Buf() for _ in range(4)]
        SQHs = [self.carve(20480 + i * 256, 256, BF16) for i in range(4)]
        SQHB = [Buf() for _ in range(4)]
        STAT = self.carve(22016, 48).rearrange("p (h t) -> p h t", h=12)
        STATB = Buf()
        M2 = self.carve(22064, 12)
        M2b = self.carve(22076, 8, BF16)
        MS = self.carve(22084, 32)
        PROD = self.carve(22116, 16)
        NEGC = self.carve(22132, 16)
        ESK = self.carve(22148, 16)
        ESP = self.carve(22164, 8)
        SMB = Buf()
        PT = self.carve(16384, 768, BF16).rearrange("p (b h j q) -> p b h j q", b=2, h=2, j=3)
        PTB = [[Buf(), Buf()], [Buf(), Buf()]]
        RD = self.carve(17152, 256).rearrange("p (b q) -> p b q", b=2)
        RDB = [Buf(), Buf()]
        S.dma("sp", I("dma_start", out=CS[:, 0, :], in_=self.d_cos[:, :]), writes=[CSB])
        S.dma("sp", I("dma_start", out=CS[:, 1, :], in_=self.d_sin[:, :]), writes=[CSB])
        rot = self.cmat("rot")
        ND1 = 0
        for oc in range(12):
            def evac(banks, oc=oc):
                for t in range(NTB):
                    b = banks[t]
                    ts = slice(t * TB, (t + 1) * TB)
                    S.op("act", I("copy", out=QBs[t], in_=PS[:, b, :]), reads=[PB[b]], writes=[QBB[t]])
                    S.op("dve", I("tensor_tensor", out=QFs[t], in0=PS[:, b, :], in1=CS[:, 0, ts], op=ALU.mult),
                         reads=[PB[b], CSB], writes=[QFB[t]])
                for t in range(NTB):
                    b = banks[t]
                    S.op("pe", I("matmul", out=PS[:, b, :], lhsT=rot, rhs=QBs[t], start=True, stop=True),
                         reads=[self.cstB, QBB[t]], writes=[PB[b]])
                for t in range(NTB):
                    b = banks[t]
                    ts = slice(t * TB, (t + 1) * TB)
                    S.op("dve", I("tensor_tensor", out=QF2s[t], in0=PS[:, b, :], in1=CS[:, 1, ts], op=ALU.mult),
                         reads=[PB[b], CSB], writes=[QF2B[t]])
                    S.op("pool", I("tensor_tensor", out=QR[:, oc, ts], in0=QFs[t], in1=QF2s[t], op=ALU.add),
                         reads=[QFB[t], QF2B[t]], writes=[QRB[oc][t]])
            self.gemm_all(evac)
            for _ in range(ND1):
                bdm = (self.gcount % 2) * 4 + 3
                S.op("pe", I("matmul", out=PS[:, bdm, :], lhsT=rot, rhs=self.XT[:, 0, 0:512], start=True, stop=True),
                     reads=[self.cstB], writes=[PB[bdm]], inc=False)
        S.barrier()
        for vh in range(2):
            pset = self.gcount % 2
            self.gcount += 1
            banks = [pset * 4 + i for i in range(4)]
            sl, wb = self.wq_get()
            slv = sl.rearrange("p (k c) -> p k c", k=8)
            for tt in range(16):
                b = banks[tt // 4]
                col = (tt % 4) * 128
                for kc in range(8):
                    last = (kc == 7 and tt % 4 == 3)
                    S.op("pe", I("matmul", out=PS[:, b, col:col + 128], lhsT=self.XT[:, kc, tt * 128:(tt + 1) * 128],
                                 rhs=slv[:, kc, :], start=(kc == 0), stop=(kc == 7)),
                         reads=[wb, self.XB[kc][tt // 4]], writes=[PB[b]], inc=last)
            for bi in range(4):
                b = banks[bi]
                src = PS[:, b, :].rearrange("p (t g d) -> p t g d", t=4, g=2)
                wr = [VDB[bi * 4 + i] for i in range(4)]
                dst0 = VD[:, bi * 4:(bi + 1) * 4, vh * 2:vh * 2 + 2, 0:64]
                dst1 = VD[:, bi * 4:(bi + 1) * 4, vh * 2:vh * 2 + 2, 64:128]
                S.op("act", I("copy", out=dst0, in_=src), reads=[PB[b]], writes=wr)
                S.op("dve", I("tensor_copy", out=dst1, in_=src), reads=[PB[b]], writes=wr)
        bd = self.cmat("bdiag")
        for oc in range(12):
            for t in range(NTB):
                ts = slice(t * TB, (t + 1) * TB)
                S.op("act", I("activation", out=SQHs[t], in_=QR[:, oc, ts], func=AF.Square), reads=[QRB[oc][t]], writes=[SQHB[t]])
                b = (self.gcount % 8)
                self.gcount += 1
                S.op("pe", I("matmul", out=PS[:, b, :], lhsT=bd, rhs=SQHs[t], start=True, stop=True),
                     reads=[self.cstB, SQHB[t]], writes=[PB[b]])
                S.op("dve", I("reduce_max", out=STAT[:, oc, t:t + 1], in_=PS[:, b, :], axis=AX.X), reads=[PB[b]], writes=[STATB])
        S.op("dve", I("reduce_max", out=M2, in_=STAT, axis=AX.X), reads=[STATB], writes=[SMB])
        S.op("dve", I("tensor_copy", out=M2b[:, 0:12], in_=M2), reads=[SMB], writes=[SMB])
        b = self.gcount % 8
        self.gcount += 1
        S.op("pe", I("matmul", out=PS[:, b, 0:12], lhsT=self.cmat("e0"), rhs=M2b[:, 0:12], start=True, stop=True),
             reads=[self.cstB, SMB], writes=[PB[b]])
        S.op("pe", I("matmul", out=PS[:, b, 16:28], lhsT=self.cmat("e1"), rhs=M2b[:, 0:12], start=True, stop=True),
             reads=[self.cstB, SMB], writes=[PB[b]])
        S.op("dve", I("tensor_copy", out=MS, in_=PS[:, b, 0:32]), reads=[PB[b]], writes=[SMB])
        PRODv = PROD.rearrange("p (g o two) -> p g o two", g=4, o=2, two=2)
        KBC = MS[:, 8:12].rearrange("p (g o) -> p g o", o=1).to_broadcast([128, 4, 2])
        S.op("dve", I("tensor_tensor", out=PRODv[:, :, :, 0], in0=MS[:, 0:8].rearrange("p (g o) -> p g o", o=2), in1=KBC, op=ALU.mult),
             reads=[SMB], writes=[SMB])
        S.op("dve", I("tensor_tensor", out=PRODv[:, :, :, 1], in0=MS[:, 16:24].rearrange("p (g o) -> p g o", o=2), in1=KBC, op=ALU.mult),
             reads=[SMB], writes=[SMB])
        S.op("act", I("activation", out=PROD, in_=PROD, func=AF.Ln), reads=[SMB], writes=[SMB])
        S.op("act", I("activation", out=PROD, in_=PROD, func=AF.Exp, scale=0.5), reads=[SMB], writes=[SMB])
        S.op("dve", I("tensor_scalar", out=NEGC, in0=PROD, scalar1=-0.125 / 64.0 * 1.01, scalar2=None, op0=ALU.mult), reads=[SMB], writes=[SMB])
        so = PAR["sinks"] + a * 16
        NEGCP = self.carve(22172, 8)
        NEGCv = NEGC.rearrange("p (g two hi) -> p g two hi", g=4, two=2)
        NEGCPv = NEGCP.rearrange("p (g hi) -> p g hi", g=4)
        S.op("dve", I("tensor_tensor", out=NEGCPv, in0=NEGCv[:, :, 0, :], in1=NEGCv[:, :, 1, :], op=ALU.min), reads=[SMB], writes=[SMB])
        S.op("dve", I("tensor_tensor", out=ESK.rearrange("p (g two hi) -> p g two hi", g=4, two=2),
                      in0=self.par[:, so:so + 16].rearrange("p (g two hi) -> p g two hi", g=4, two=2),
                      in1=NEGCPv.rearrange("p g (o hi) -> p g o hi", o=1).to_broadcast([128, 4, 2, 2]), op=ALU.add),
             reads=[SMB, self.parB], writes=[SMB])
        S.op("act", I("activation", out=ESK, in_=ESK, func=AF.Exp), reads=[SMB], writes=[SMB])
        ESKv = ESK.rearrange("p (c two) -> p c two", two=2)
        S.op("dve", I("tensor_copy", out=ESP[0:64, :], in_=ESKv[0:64, :, 0]), reads=[SMB], writes=[SMB])
        S.op("dve", I("tensor_copy", out=ESP[64:128, :], in_=ESKv[64:128, :, 1]), reads=[SMB], writes=[SMB])
        ESPb = self.carve(22180, 4, BF16)
        ESPT = self.carve(22184, 64, BF16)
        S.op("dve", I("tensor_copy", out=ESPb[:, 0:8], in_=ESP), reads=[SMB], writes=[SMB])
        bt_ = self.gcount % 8
        self.gcount += 1
        pbt = PS[:, bt_, :].bitcast(BF16)
        S.op("pe", I("transpose", out=pbt[0:8, 0:128], in_=ESPb[:, 0:8], identity=self.cmat("ident")), reads=[SMB, self.cstB], writes=[PB[bt_]])
        S.op("dve", I("tensor_copy", out=ESPT[0:8, :], in_=pbt[0:8, 0:128]), reads=[PB[bt_]], writes=[SMB])
        S.barrier()
        PT = self.carve(16384, 768, BF16).rearrange("p (b j h q) -> p b j h q", b=2, j=3, h=2)
        PTB = [[Buf(), Buf()], [Buf(), Buf()]]
        RD = self.carve(17152, 512).rearrange("p (b q) -> p b q", b=2)
        RDB = [Buf(), Buf()]
        SEL = self.carve(17664, 512, BF16).rearrange("p (c q) -> p c q", c=8)
        S.op("dve", I("tensor_copy", out=SEL[0:8, :, :],
                      in_=self.cmat("ident")[0:8, 0:8].rearrange("p (c o) -> p c o", o=1).to_broadcast([8, 8, 128])),
             reads=[self.cstB], writes=[SMB])
        nmprev = self.cmat("nmprev").rearrange("p (o q) -> p o q", o=1).to_broadcast([128, 2, 128])
        nmnext = self.cmat("nmnext").rearrange("p (o q) -> p o q", o=1).to_broadcast([128, 2, 128])
        identm = self.cmat("ident")
        onesm = self.cmat("ones")
        OB = [[[Buf() for _ in range(2)] for _ in range(4)] for _ in range(16)]
        it = 0
        NDUMMY = 4
        for n in range(16):
            js = [j for j in (n - 1, n, n + 1) if 0 <= j < 16]
            qs = slice(n * 128, (n + 1) * 128)
            for g in range(4):
                for hi in range(2):
                    sb_ = it % 2
                    it += 1
                    bA, bB, bO = sb_ * 3, sb_ * 3 + 1, sb_ * 3 + 2
                    pr = slice(64 * hi, 64 * hi + 64)
                    pidx = g * 2 + hi
                    qmov = QR[pr, 2 * g:2 * g + 2, qs]
                    qbufs = [QRB[2 * g][n // 4], QRB[2 * g + 1][n // 4]]

                    def sc_out(jj):
                        if jj < 2:
                            return PS[:, bA, jj * 256:(jj + 1) * 256].rearrange("p (h q) -> p h q", h=2), bA
                        return PS[:, bB, 0:256].rearrange("p (h q) -> p h q", h=2), bB
                    for j in js:
                        jj = j - (n - 1)
                        o_ap, bk = sc_out(jj)
                        msk = None if jj == 1 else (nmprev if jj == 0 else nmnext)
                        S.op("pe", I("matmul", out=o_ap, lhsT=QR[pr, 8 + g, j * 128:(j + 1) * 128], rhs=qmov, start=True, stop=(msk is None)),
                             reads=[QRB[8 + g][j // 4]] + qbufs, writes=[PB[bk]], inc=(msk is None))
                        if msk is not None:
                            S.op("pe", I("matmul", out=o_ap, lhsT=identm, rhs=msk, start=False, stop=True),
                                 reads=[self.cstB], writes=[PB[bk]])
                    for _ in range(NDUMMY):
                        S.op("pe", I("matmul", out=PS[:, 6 + (it % 2), :], lhsT=identm, rhs=QR[:, 8, 0:512], start=True, stop=True),
                             reads=[self.cstB], writes=[PB[6 + (it % 2)]], inc=False)
                    pt = PT[:, sb_, :, :, :]
                    bias = NEGCP[:, pidx:pidx + 1]
                    jA = [j - (n - 1) for j in js if j - (n - 1) < 2]
                    if jA:
                        S.op("act", I("activation", out=pt[:, jA[0]:jA[-1] + 1, :, :],
                                      in_=PS[:, bA, jA[0] * 256:(jA[-1] + 1) * 256].rearrange("p (j h q) -> p j h q", j=len(jA), h=2),
                                      func=AF.Exp, bias=bias, scale=self.dcol(3)),
                             reads=[PB[bA], SMB, self.derB], writes=[PTB[sb_][0]])
                    if n + 1 < 16:
                        S.op("act", I("activation", out=pt[:, 2, :, :], in_=PS[:, bB, 0:256].rearrange("p (h q) -> p h q", h=2),
                                      func=AF.Exp, bias=bias, scale=self.dcol(3)),
                             reads=[PB[bB], SMB, self.derB], writes=[PTB[sb_][1]])
                    o_out = PS[:, bO, 0:256].rearrange("p (h q) -> p h q", h=2)
                    d_out = PS[:, bO, 256:512].rearrange("p (h q) -> p h q", h=2)
                    for j in js:
                        jj = j - (n - 1)
                        S.op("pe", I("matmul", out=o_out, lhsT=VD[:, j, g, :], rhs=pt[:, jj, :, :], start=(j == js[0]), stop=(j == js[-1])),
                             reads=[VDB[j], PTB[sb_][jj // 2]], writes=[PB[bO]], inc=False)
                    for j in js:
                        jj = j - (n - 1)
                        S.op("pe", I("matmul", out=d_out, lhsT=onesm, rhs=pt[:, jj, :, :], start=(j == js[0]), stop=False),
                             reads=[self.cstB, PTB[sb_][jj // 2]], writes=[PB[bO]], inc=False)
                    S.op("pe", I("matmul", out=d_out, lhsT=ESPT[0:8, :], rhs=SEL[0:8, 2 * g:2 * g + 2, :], start=False, stop=True),
                         reads=[SMB], writes=[PB[bO]])
                    rd = RD[:, sb_, :].rearrange("p (h q) -> p h q", h=2)
                    S.op("dve", I("reciprocal", out=rd[pr], in_=d_out[pr]), reads=[PB[bO]], writes=[RDB[sb_]])
                    S.op("dve", I("tensor_tensor", out=self.XT[pr, 2 * g:2 * g + 2, qs], in0=o_out[pr], in1=rd[pr], op=ALU.mult),
                         reads=[PB[bO], RDB[sb_]], writes=[OB[n][g][hi]])
        S.barrier()
        self.out_proj(layer, gnext)


_PROG_CACHE = {}


def _get_prog(subs):
    key = tuple(subs)
    if key not in _PROG_CACHE:
        _PROG_CACHE[key] = Prog(subs)
    return _PROG_CACHE[key]


def _run(subs, xT_list, shared):
    prog = _get_prog(subs)
    in_maps = []
    for xT in xT_list:
        m = dict(shared)
        m["xT"] = xT
        in_maps.append(m)
    res = run_bass_kernel_spmd(prog.nc, in_maps, core_ids=list(range(len(xT_list))))
    return [r["yT"] for r in res.results]


def _shared_inputs(inp):
    cosT, sinT = _rope_tables()
    shared = {
        "par": _pack_params(inp["norm_g"], inp["rec_conv_w"], inp["rec_conv_b"], inp["rg_b_r"], inp["rg_b_i"],
                            inp["rg_lambda"], inp["hgrn_lb_logits"], inp["hgrn_norm_g"], inp["att_sinks"]),
        "cst": _make_consts(),
        "cosT": cosT,
        "sinT": sinT,
    }
    for k in ("rec_w_in", "rg_w_r", "rg_w_i", "rec_w_out", "att_w_qkv", "att_w_o", "mlp_w1", "mlp_w2"):
        shared[k] = np.ascontiguousarray(np.asarray(inp[k], np.float32))
    return shared


LAUNCH_GROUPS = [list(range(8))]


def kernel(**inp):
    x = np.asarray(inp["x"], np.float32)
    B = x.shape[0]
    shared = _shared_inputs(inp)
    cur = [np.ascontiguousarray(x[b].T) for b in range(B)]
    for subs in LAUNCH_GROUPS:
        cur = _run(subs, cur, shared)
    out = np.stack([np.ascontiguousarray(c.T) for c in cur], axis=0)
    return out.astype(np.float32)
```

```python
import contextlib
import numpy as np
import concourse.bass as bass
import concourse.mybir as mybir
from concourse.bass_utils import run_bass_kernel_spmd

F32 = mybir.dt.float32
BF16 = mybir.dt.bfloat16
AF = mybir.ActivationFunctionType
ALU = mybir.AluOpType
AX = mybir.AxisListType

S_LEN = 2048
D = 1024
NCH = 8
TB = 512
NTB = 4
DFF = 4096
EPS = 1e-6
ENGS = ("pe", "act", "dve", "pool", "sp")


class Buf:
    __slots__ = ("name", "w", "r", "excl")

    def __init__(self, name="", excl=False):
        self.name = name
        self.w = None
        self.r = {}
        self.excl = excl


def I(name, **kw):
    return (name, kw)


class Sched:
    def __init__(self, nc):
        self.nc = nc
        self.sems = {}
        self.cnt = {}
        self.clockof = {}
        self.know = {e: {} for e in ENGS}
        self.prog = {e: [] for e in ENGS}
        self._ctx = []
        for e in ENGS:
            self._newsrc(e)
        self.nchan = 0

    def _newsrc(self, name):
        cm = self.nc.semaphore("s_" + name)
        sem = cm.__enter__()
        self._ctx.append(cm)
        self.sems[name] = sem
        self.cnt[name] = 0
        return sem

    def close(self):
        for cm in reversed(self._ctx):
            cm.__exit__(None, None, None)

    def _need(self, e, toks):
        k = self.know[e]
        waits = {}
        for t in toks:
            if t is None:
                continue
            s, c = t
            if s == e and e == "pe":
                continue
            if k.get(s, 0) >= c:
                continue
            if waits.get(s, 0) < c:
                waits[s] = c
        for s, c in waits.items():
            clk = self.clockof.get((s, c))
            if clk:
                for s2, c2 in clk.items():
                    if k.get(s2, 0) < c2:
                        k[s2] = c2
            if k.get(s, 0) < c:
                k[s] = c
        return list(waits.items())

    def _deps(self, reads, writes):
        toks = []
        for b in reads:
            toks.append(b.w)
            if b.excl:
                for s, c in b.r.items():
                    toks.append((s, c))
        for b in writes:
            toks.append(b.w)
            for s, c in b.r.items():
                toks.append((s, c))
        return toks

    muted = False

    def op(self, e, fn, reads=(), writes=(), inc=True):
        if self.muted:
            return None
        waits = self._need(e, self._deps(reads, writes))
        if inc:
            self.cnt[e] += 1
            tok = (e, self.cnt[e])
            self.clockof[tok] = dict(self.know[e])
        else:
            tok = (e, self.cnt[e] + 1)
        self.prog[e].append((waits, fn, e if inc else None, 1))
        for b in reads:
            if b.r.get(e, 0) < tok[1]:
                b.r[e] = tok[1]
        for b in writes:
            b.w = tok
            b.r = {}
        return tok

    def dma(self, q, fn, reads=(), writes=(), chan=None):
        if self.muted:
            return None
        if chan is None:
            chan = "c%d" % self.nchan
            self.nchan += 1
        if chan not in self.sems:
            self._newsrc(chan)
        waits = self._need(q, self._deps(reads, writes))
        self.cnt[chan] += 16
        tok = (chan, self.cnt[chan])
        self.clockof[tok] = dict(self.know[q])
        self.prog[q].append((waits, fn, chan, 16))
        for b in reads:
            if b.r.get(chan, 0) < tok[1]:
                b.r[chan] = tok[1]
        for b in writes:
            b.w = tok
            b.r = {}
        return tok

    def wait_all(self, e, toks):
        waits = self._need(e, toks)
        self.prog[e].append((waits, None, None, 0))

    def barrier(self, pe_waits=False):
        if self.muted:
            return
        toks = [(e, self.cnt[e]) for e in ("pe", "act", "dve", "pool") if self.cnt[e] > 0]
        toks += [(ch, c) for ch, c in self.cnt.items() if ch not in ENGS and not ch.startswith("wl") and c > 0]
        for e in ("pe", "act", "dve", "pool", "sp"):
            if e == "pe" and not pe_waits:
                continue
            self.wait_all(e, toks)

    def emit(self):
        nc = self.nc
        sems = self.sems
        prog = self.prog
        with nc.Block() as block:
            def run(name, engobj):
                for waits, fn, incsrc, amt in prog[name]:
                    for s, c in waits:
                        engobj.wait_ge(sems[s], c)
                    if fn is not None:
                        if isinstance(fn, tuple):
                            ins = getattr(engobj, fn[0])(**fn[1])
                        else:
                            ins = fn(engobj)
                        if incsrc is not None:
                            ins.then_inc(sems[incsrc], amt)

            @block.tensor
            def _(eng):
                run("pe", eng)

            @block.scalar
            def _(eng):
                run("act", eng)

            @block.vector
            def _(eng):
                run("dve", eng)

            @block.gpsimd
            def _(eng):
                run("pool", eng)

            @block.sync
            def _(eng):
                run("sp", eng)


def _par_layout():
    off = {}
    o = 0

    def add(name, n):
        nonlocal o
        off[name] = o
        o += n
    add("norm_g", 4 * 4 * 8)
    add("conv_w", 2 * 4 * 4)
    add("conv_b", 2 * 4)
    add("b_r", 2 * 2 * 4)
    add("b_i", 2 * 2 * 4)
    add("lam", 2 * 2 * 4)
    add("lbl", 2 * 2 * 4)
    add("hng", 2 * 4)
    add("sinks", 2 * 16)
    off["_n"] = o
    return off


PAR = _par_layout()


def _pack_params(norm_g, rec_conv_w, rec_conv_b, rg_b_r, rg_b_i, rg_lambda, hgrn_lb_logits,
                 hgrn_norm_g, att_sinks):
    par = np.zeros((128, PAR["_n"]), np.float32)

    def put(name, arr):
        a = np.asarray(arr, np.float32)
        lead = int(np.prod(a.shape[:-1]))
        n = a.shape[-1] // 128
        a = a.reshape(lead, n, 128).transpose(2, 0, 1).reshape(128, lead * n)
        par[:, PAR[name]:PAR[name] + lead * n] = a
    put("norm_g", norm_g)
    put("conv_w", rec_conv_w)
    put("conv_b", rec_conv_b)
    put("b_r", rg_b_r)
    put("b_i", rg_b_i)
    put("lam", rg_lambda)
    put("lbl", hgrn_lb_logits)
    put("hng", hgrn_norm_g)
    par[:, PAR["sinks"]:PAR["sinks"] + 32] = np.asarray(att_sinks, np.float32).reshape(1, 32)
    return par


def _cst_layout():
    off = {}
    o = 0

    def add(name, n):
        nonlocal o
        off[name] = o
        o += n
    add("ones", 128)
    add("ident", 128)
    add("rot", 128)
    add("e0", 128)
    add("e1", 128)
    add("mprev", 128)
    add("mnext", 128)
    add("hmf", 128)
    add("hmb", 128)
    add("cmask", 64)
    add("onesL", 128)
    add("onesR", 128)
    add("bdiag", 128)
    add("nmprev", 128)
    add("nmnext", 128)
    off["_n"] = o
    return off


CST = _cst_layout()


def _make_consts():
    c = np.zeros((128, CST["_n"]), np.float32)
    i = np.arange(128)
    c[:, CST["ones"]:CST["ones"] + 128] = 1.0
    c[i, CST["ident"] + i] = 1.0
    R = np.zeros((128, 128), np.float32)
    for m in range(128):
        hb = (m // 64) * 64
        r = m % 64
        if r < 32:
            R[hb + r + 32, m] = -1.0
        else:
            R[hb + r - 32, m] = 1.0
    c[:, CST["rot"]:CST["rot"] + 128] = R
    c[:64, CST["e0"]:CST["e0"] + 128] = 1.0
    c[64:, CST["e1"]:CST["e1"] + 128] = 1.0
    b = i[:, None]
    a = i[None, :]
    c[:, CST["mprev"]:CST["mprev"] + 128] = (a <= b)
    c[:, CST["mnext"]:CST["mnext"] + 128] = (b <= a)
    c[:, CST["nmprev"]:CST["nmprev"] + 128] = np.where(a <= b, 0.0, -30000.0)
    c[:, CST["nmnext"]:CST["nmnext"] + 128] = np.where(b <= a, 0.0, -30000.0)
    same = (b // 64) == (a // 64)
    c[:, CST["hmf"]:CST["hmf"] + 128] = same & (b <= a)
    c[:, CST["hmb"]:CST["hmb"] + 128] = same & (b >= a)
    c[:, CST["onesL"]:CST["onesL"] + 64] = 1.0
    c[:, CST["onesR"] + 64:CST["onesR"] + 128] = 1.0
    c[:64, CST["bdiag"]:CST["bdiag"] + 64] = 1.0
    c[64:, CST["bdiag"] + 64:CST["bdiag"] + 128] = 1.0
    cm = np.ones(64, np.float32)
    cm[0] = 0.0
    c[:, CST["cmask"]:CST["cmask"] + 64] = cm[None, :]
    return c


def _rope_tables():
    pos = np.arange(S_LEN, dtype=np.float32)
    inv_freq = (np.float32(10000.0) ** (-np.arange(0, 64, 2, dtype=np.float32) / np.float32(64))).astype(np.float32)
    ang = (pos[:, None] * inv_freq[None, :]).astype(np.float32)
    cos = np.cos(ang).astype(np.float32).T
    sin = np.sin(ang).astype(np.float32).T
    cosT = np.tile(cos, (4, 1))
    sinT = np.tile(sin, (4, 1))
    return np.ascontiguousarray(cosT), np.ascontiguousarray(sinT)


class Prog:
    NSLOT = 8

    def __init__(self, sublayers):
        self.sublayers = list(sublayers)
        nc = bass.Bass("TRN2", target_bir_lowering=False)
        self.nc = nc
        self.es = contextlib.ExitStack()
        dr = lambda name, shape, kind="ExternalInput": nc.dram_tensor(name, shape, F32, kind=kind).ap()
        self.d_x = dr("xT", [D, S_LEN])
        self.d_y = dr("yT", [D, S_LEN], "ExternalOutput")
        self.d_par = dr("par", [128, PAR["_n"]])
        self.d_cst = dr("cst", [128, CST["_n"]])
        self.d_cos = dr("cosT", [128, S_LEN])
        self.d_sin = dr("sinT", [128, S_LEN])
        self.d_yscr = nc.dram_tensor("yscr", [D, S_LEN], BF16, kind="Internal").ap()
        self.d_w = {
            "rec_w_in": dr("rec_w_in", [2, D, 3584]),
            "rg_w_r": dr("rg_w_r", [2, 2, 4, 128, 128]),
            "rg_w_i": dr("rg_w_i", [2, 2, 4, 128, 128]),
            "rec_w_out": dr("rec_w_out", [2, D, D]),
            "att_w_qkv": dr("att_w_qkv", [2, D, 1536]),
            "att_w_o": dr("att_w_o", [2, D, D]),
            "mlp_w1": dr("mlp_w1", [4, D, DFF]),
            "mlp_w2": dr("mlp_w2", [4, DFF, D]),
        }
        self.S = Sched(nc)
        self._alloc()
        self._build()
        self.S.emit()
        self.es.close()
        self.S.close()

    def sb(self, name, shape, dt=F32):
        return self.es.enter_context(self.nc.sbuf_tensor("sb_" + name, shape, dt))

    def _alloc(self):
        nc = self.nc
        self.hT = self.sb("hT", [128, NCH, S_LEN])
        self.HB = [[Buf("h%d_%d" % (c, t)) for t in range(NTB)] for c in range(NCH)]
        self.XT = self.sb("XT", [128, NCH, S_LEN], BF16)
        self.XB = [[Buf("x%d_%d" % (c, t)) for t in range(NTB)] for c in range(NCH)]
        self.wring = self.sb("wring", [128, self.NSLOT, 1024], BF16)
        self.WB = [Buf("w%d" % i) for i in range(self.NSLOT)]
        self.par = self.sb("par", [128, PAR["_n"]])
        self.parB = Buf("par")
        self.cst = self.sb("cst", [128, CST["_n"]], BF16)
        self.cstB = Buf("cst")
        self.der = self.sb("der", [128, 256])
        self.derB = Buf("der")
        self.ARENA_W = 22528
        self.AR = self.sb("arena", [128, self.ARENA_W])
        self.EB = self.ARENA_W - 7168
        self.MT = self.carve(self.EB, 4096).rearrange("p (c t) -> p c t", c=NCH)
        self.MB = [Buf("m%d" % c) for c in range(NCH)]
        self.SQ = self.carve(self.EB + 4096, 2048, BF16).rearrange("p (c t) -> p c t", c=NCH)
        self.SQB = [Buf("sq%d" % c) for c in range(NCH)]
        self.RS = self.carve(self.EB + 6144, 1024).rearrange("p (c t) -> p c t", c=2)
        self.RSB = [Buf("rs0"), Buf("rs1")]
        self.SQ2 = self.carve(12288, 2048, BF16).rearrange("p (c t) -> p c t", c=NCH)
        self.SQ2B = [Buf("sq2_%d" % c) for c in range(NCH)]
        self.PS = self.es.enter_context(nc.psum_tensor("ps", [128, 8, 512], F32))
        self.PB = [Buf("ps%d" % i, excl=True) for i in range(8)]
        self.stream_out = True
        self.handoff = True
        self.first_gemm_t_outer = False
        self.carry = None
        self.stored = set()
        self.out_toks = []
        self.wq_items = []
        self.wq_next_issue = 0
        self.wq_next_use = 0
        self.gcount = 0

    def carve(self, off_words, nwords, dt=F32):
        ap = self.AR[:, off_words:off_words + nwords]
        if dt == BF16:
            ap = ap.bitcast(BF16)
        return ap

    def pcol(self, name, idx):
        o = PAR[name] + idx
        return self.par[:, o:o + 1]

    def cmat(self, name, n=128):
        o = CST[name]
        return self.cst[:, o:o + n]

    def wq_plan(self, items):
        self.wq_items.extend(items)

    def _wq_issue(self, i):
        slot = i % self.NSLOT
        sl = self.wring[:, slot, :]
        for view_fn, src in self.wq_items[i]:
            dst = view_fn(sl)
            self.S.dma("pool", (lambda dst, src: lambda e: e.dma_start(out=dst, in_=src))(dst, src),
                       writes=[self.WB[slot]], chan="wl%d" % slot)

    def wq_get(self):
        i = self.wq_next_use
        self.wq_next_use += 1
        lim = min(i + self.NSLOT, len(self.wq_items))
        while self.wq_next_issue < lim:
            self._wq_issue(self.wq_next_issue)
            self.wq_next_issue += 1
        slot = i % self.NSLOT
        return self.wring[:, slot, :], self.WB[slot]

    def pieces_tb(self, W, KC, c0):
        items = []
        for k2 in range(KC // 2):
            src = W[k2 * 256:(k2 + 1) * 256, c0:c0 + 512].rearrange("(i p) c -> p i c", p=128)
            items.append([(lambda sl: sl.rearrange("p (i c) -> p i c", i=2), src)])
        return items

    def piece_cols(self, W, c0):
        src = W[:, c0:c0 + 128].rearrange("(k p) c -> p k c", p=128)
        return [[(lambda sl: sl.rearrange("p (k c) -> p k c", k=8), src)]]

    def gemm_tb(self, KC, rhs_fn, rhs_bufs_fn, evac, defer_evac=False):
        S = self.S
        pset = self.gcount % 2
        self.gcount += 1
        banks = [pset * 4 + i for i in range(4)]
        for k2 in range(KC // 2):
            sl, wb = self.wq_get()
            slv = sl.rearrange("p (i c) -> p i c", i=2)
            for i in range(2):
                kc = k2 * 2 + i
                for oc in range(4):
                    last = (i == 1 and oc == 3)
                    S.op("pe", (lambda b, l, r, st, sp: lambda e: e.matmul(self.PS[:, b, :], lhsT=l, rhs=r, start=st, stop=sp))(
                        banks[oc], slv[:, i, oc * 128:(oc + 1) * 128], rhs_fn(kc), kc == 0, kc == KC - 1),
                        reads=[wb] + rhs_bufs_fn(kc), writes=[self.PB[banks[oc]]], inc=last)
        def do_evac():
            for oc in range(4):
                evac(oc, self.PS[:, banks[oc], :], self.PB[banks[oc]])
        if defer_evac:
            return do_evac
        do_evac()

    def gemm_all(self, evac, defer_evac=False):
        S = self.S
        pset = self.gcount % 2
        self.gcount += 1
        banks = [pset * 4 + i for i in range(4)]
        sl, wb = self.wq_get()
        slv = sl.rearrange("p (k c) -> p k c", k=8)
        order = [(kc, t) for kc in range(8) for t in range(NTB)]
        if self.first_gemm_t_outer:
            order = [(kc, t) for t in range(NTB) for kc in range(8)]
            self.first_gemm_t_outer = False
        for idx, (kc, t) in enumerate(order):
            last = (idx == len(order) - 1)
            S.op("pe", I("matmul", out=self.PS[:, banks[t], :], lhsT=slv[:, kc, :], rhs=self.XT[:, kc, t * TB:(t + 1) * TB],
                         start=(kc == 0), stop=(kc == 7)),
                 reads=[wb, self.XB[kc][t]], writes=[self.PB[banks[t]]], inc=last)
        if defer_evac:
            return (lambda: evac(banks)), pset
        evac(banks)

    def rstd_from_sq(self, which, nchunks=NCH, dim=D, sq=None, sqb=None):
        S = self.S
        bank = 0 if (self.gcount % 2 == 0) else 4
        sq = self.SQ if sq is None else sq
        sqb = self.SQB if sqb is None else sqb
        for c in range(nchunks):
            S.op("pe", I("matmul", out=self.PS[:, bank, :], lhsT=self.cmat("ones"), rhs=sq[:, c, :], start=(c == 0), stop=(c == nchunks - 1)),
                 reads=[self.cstB, sqb[c]], writes=[self.PB[bank]], inc=(c == nchunks - 1))
        rs = self.RS[:, which, :]
        S.op("act", lambda e: e.activation(out=rs, in_=self.PS[:, bank, :], func=AF.Ln, bias=self.der[:, 0:1], scale=(self.der[:, 1:2] if dim == D else self.der[:, 2:3])),
             reads=[self.PB[bank], self.derB], writes=[self.RSB[which]])
        S.op("act", lambda e: e.activation(out=rs, in_=rs, func=AF.Exp, scale=-0.5),
             reads=[self.RSB[which]], writes=[self.RSB[which]])

    def norm_to_XT(self, tb, gidx):
        S = self.S
        ts = slice(tb * TB, (tb + 1) * TB)
        for c in range(NCH):
            S.op("act", I("activation", out=self.SQ2[:, c, :], in_=self.hT[:, c, ts], func=AF.Square),
                 reads=[self.HB[c][tb]], writes=[self.SQ2B[c]])
        self.rstd_from_sq(1, sq=self.SQ2, sqb=self.SQ2B)
        for c in range(NCH):
            S.op("dve", (lambda c: lambda e: e.scalar_tensor_tensor(
                out=self.XT[:, c, ts], in0=self.hT[:, c, ts], scalar=self.pcol("norm_g", gidx * 8 + c),
                in1=self.RS[:, 1, :], op0=ALU.mult, op1=ALU.mult))(c),
                reads=[self.HB[c][tb], self.RSB[1], self.parB], writes=[self.XB[c][tb]])

    def post_block(self, tb, gpost, gnext):
        self.post_block_a(tb, gpost)
        if gnext is not None:
            self.norm_to_XT(tb, gnext)
        elif self.stream_out:
            self.store_tb(tb)

    def store_tb(self, tb):
        yv = self.d_y.rearrange("(c p) t -> p c t", p=128)
        ts = slice(tb * TB, (tb + 1) * TB)
        for c in range(NCH):
            self.out_toks.append(self.S.dma("sp", I("dma_start", out=yv[:, c, ts], in_=self.hT[:, c, ts]),
                                            reads=[self.HB[c][tb]], chan="outc"))
        self.stored.add(tb)

    def post_block_a(self, tb, gpost):
        S = self.S
        ts = slice(tb * TB, (tb + 1) * TB)
        self.rstd_from_sq(0)
        for c in range(NCH):
            S.op("dve", (lambda c: lambda e: e.scalar_tensor_tensor(
                out=self.MT[:, c, :], in0=self.MT[:, c, :], scalar=self.pcol("norm_g", gpost * 8 + c),
                in1=self.RS[:, 0, :], op0=ALU.mult, op1=ALU.mult))(c),
                reads=[self.MB[c], self.RSB[0], self.parB], writes=[self.MB[c]])
            S.op("pool" if c % 2 == 0 else "dve",
                 I("tensor_tensor", out=self.hT[:, c, ts], in0=self.hT[:, c, ts], in1=self.MT[:, c, :], op=ALU.add),
                 reads=[self.MB[c], self.HB[c][tb]], writes=[self.HB[c][tb]])

    def evac_to_MT(self, c):
        S = self.S

        def f(oc, bank_ap, bank_buf):
            S.op("dve", lambda e: e.tensor_copy(out=self.MT[:, c, :], in_=bank_ap), reads=[bank_buf], writes=[self.MB[c]])
            S.op("act", lambda e: e.activation(out=self.SQ[:, c, :], in_=self.MT[:, c, :], func=AF.Square),
                 reads=[self.MB[c]], writes=[self.SQB[c]])
        return f

    def out_proj(self, layer, gnext):
        pend = None
        for tb in range(NTB):
            ts = slice(tb * TB, (tb + 1) * TB)
            evs = []
            for g in range(2):
                def evac2(oc, bank_ap, bank_buf, g=g):
                    self.evac_to_MT(g * 4 + oc)(oc, bank_ap, bank_buf)
                evs.append(self.gemm_tb(8, (lambda ts: lambda kc: self.XT[:, kc, ts])(ts), (lambda tb: lambda kc: [self.XB[kc][tb]])(tb),
                                        evac2, defer_evac=True))
                if pend is not None and g == 0:
                    self.post_block_a(pend, layer * 4 + 1)
            evs[0]()
            if pend is not None and gnext is not None:
                self.norm_to_XT(pend, gnext)
            evs[1]()
            pend = tb
        if self.handoff and gnext is not None:
            self.carry = (pend, layer * 4 + 1, gnext)
        else:
            self.post_block(pend, layer * 4 + 1, gnext)

    def plan_mlp(self, layer):
        w1 = self.d_w["mlp_w1"][layer]
        w2 = self.d_w["mlp_w2"][layer]
        items = []
        for tb in range(NTB):
            for g in range(8):
                items += self.pieces_tb(w1, 8, g * 512)
            for g in range(2):
                items += self.pieces_tb(w2, 32, g * 512)
        self.wq_plan(items)

    def mlp(self, layer, gnext):
        S = self.S
        HID = self.carve(0, 8192, BF16).rearrange("p (c t) -> p c t", c=32)
        HIDB = [Buf("hid%d" % c) for c in range(32)]
        REL = self.carve(8192, 4096).rearrange("p (s c t) -> p s c t", s=2, c=4)
        RELB = [[Buf() for _ in range(4)] for _ in range(2)]
        pend = None
        for tb in range(NTB):
            ts = slice(tb * TB, (tb + 1) * TB)
            for g in range(8):
                if self.carry is not None and g == 1:
                    self.post_block_a(self.carry[0], self.carry[1])
                if self.carry is not None and g == 3:
                    self.norm_to_XT(self.carry[0], self.carry[2])
                    self.carry = None
                if pend is not None and g == 1:
                    self.post_block_a(pend, layer * 4 + 3)
                if pend is not None and g == 3:
                    if gnext is not None:
                        self.norm_to_XT(pend, gnext)
                    elif self.stream_out:
                        self.store_tb(pend)
                    pend = None
                pset = self.gcount % 2

                def evac1(oc, bank_ap, bank_buf, g=g, pset=pset):
                    hc = g * 4 + oc
                    S.op("act", lambda e: e.activation(out=REL[:, pset, oc, :], in_=bank_ap, func=AF.Relu),
                         reads=[bank_buf], writes=[RELB[pset][oc]])
                    S.op("dve", lambda e: e.tensor_tensor(out=HID[:, hc, :], in0=REL[:, pset, oc, :], in1=REL[:, pset, oc, :], op=ALU.mult),
                         reads=[RELB[pset][oc]], writes=[HIDB[hc]])
                self.gemm_tb(8, lambda kc: self.XT[:, kc, ts], lambda kc: [self.XB[kc][tb]], evac1)
            for g in range(2):
                def evac2(oc, bank_ap, bank_buf, g=g):
                    self.evac_to_MT(g * 4 + oc)(oc, bank_ap, bank_buf)
                self.gemm_tb(32, lambda kc: HID[:, kc, :], lambda kc: [HIDB[kc]], evac2)
            pend = tb
        self.post_block(pend, layer * 4 + 3, gnext)

    def load_inputs(self):
        S = self.S
        S.dma("sp", lambda e: e.dma_start(out=self.par[:], in_=self.d_par[:, :]), writes=[self.parB])
        cstf = self.carve(0, CST["_n"])
        S.dma("sp", lambda e: e.dma_start(out=cstf, in_=self.d_cst[:, :]), writes=[self.cstB])
        S.op("dve", lambda e: e.tensor_copy(out=self.cst[:], in_=cstf), reads=[self.cstB], writes=[self.cstB])
        S.op("dve", lambda e: e.memset(self.der[:, 0:1], EPS), writes=[self.derB])
        S.op("dve", lambda e: e.memset(self.der[:, 1:2], 1.0 / D), writes=[self.derB])
        S.op("dve", lambda e: e.memset(self.der[:, 2:3], 1.0 / 128), writes=[self.derB])
        S.op("dve", lambda e: e.memset(self.der[:, 3:4], 0.125), writes=[self.derB])
        for col, val in ((4, 1.0), (5, -1.0), (6, 0.5), (7, -0.5), (8, 2.0), (9, 0.0), (10, -8.0)):
            S.op("dve", I("memset", ap=self.der[:, col:col + 1], constant=val), writes=[self.derB])
        xv = self.d_x.rearrange("(c p) t -> p c t", p=128)
        for c in range(NCH):
            for tb in range(NTB):
                ts = slice(tb * TB, (tb + 1) * TB)
                S.dma("sp", (lambda c, ts: lambda e: e.dma_start(out=self.hT[:, c, ts], in_=xv[:, c, ts]))(c, ts),
                      writes=[self.HB[c][tb]])

    def store_output(self):
        S = self.S
        S.muted = False
        for tb in range(NTB):
            if tb not in self.stored:
                self.store_tb(tb)
        S.wait_all("sp", [self.out_toks[-1]])

    def _build(self):
        subs = self.sublayers
        for s in subs:
            layer, kind = s // 2, s % 2
            if kind == 1:
                self.plan_mlp(layer)
            elif layer % 2 == 0:
                self.plan_rec(layer)
            else:
                self.plan_att(layer)
        STAGE = 9
        self.load_inputs()
        self.S.barrier()
        if STAGE == 0:
            self.store_output()
            return
        first = subs[0]
        self.first_gemm_t_outer = (first % 2 == 0)
        for tb in range(NTB):
            self.norm_to_XT(tb, (first // 2) * 4 + (2 if first % 2 else 0))
        self.S.barrier()
        if STAGE == 1:
            self.store_output()
            return
        for idx, s in enumerate(subs):
            layer, kind = s // 2, s % 2
            nxt = subs[idx + 1] if idx + 1 < len(subs) else None
            gnext = None if nxt is None else (nxt // 2) * 4 + (2 if nxt % 2 else 0)
            if kind == 1:
                self.mlp(layer, gnext)
            elif layer % 2 == 0:
                self.rec(layer, gnext)
            else:
                self.att(layer, gnext)
            if self.carry is None:
                if kind == 1 and nxt is not None:
                    self.S.barrier(pe_waits=False)
                    self.first_gemm_t_outer = True
                else:
                    self.S.barrier()
        self.store_output()

    def dcol(self, i):
        return self.der[:, i:i + 1]

    def plan_rec(self, layer):
        r = layer // 2
        W = self.d_w["rec_w_in"][r]
        items = []
        items += self.piece_cols(W, 0)
        for j in range(4):
            gates = []
            for d in range(2):
                for gi, nm in enumerate(("rg_w_r", "rg_w_i")):
                    m = d * 2 + gi
                    gates.append(((lambda m: lambda sl: sl[:, m * 128:(m + 1) * 128])(m), self.d_w[nm][r, d, j]))
            items.append(gates)
            items += self.piece_cols(W, 512 + j * 128)
            if j < 3:
                items += self.piece_cols(W, (j + 1) * 128)
        items += self.piece_cols(W, 2560)
        items += self.piece_cols(W, 1024)
        for j in range(4):
            items += self.piece_cols(W, 1536 + j * 128)
            items += self.piece_cols(W, 2048 + j * 128)
            if j < 3:
                items += self.piece_cols(W, 2560 + (j + 1) * 128)
                items += self.piece_cols(W, 1024 + (j + 1) * 128)
            items += self.piece_cols(W, 3072 + j * 128)
        Wo = self.d_w["rec_w_out"][r]
        for tb in range(NTB):
            for g in range(2):
                items += self.pieces_tb(Wo, 8, g * 512)
        self.wq_plan(items)

    def evac_copy_all(self, dst, dstB, eng="act"):
        S = self.S

        dl = dstB if isinstance(dstB, list) else [dstB]

        def f(banks):
            for t in range(NTB):
                b = banks[t]
                if eng == "act":
                    S.op("act", I("copy", out=dst[:, t * TB:(t + 1) * TB], in_=self.PS[:, b, :]), reads=[self.PB[b]], writes=dl)
                else:
                    S.op("dve", I("tensor_copy", out=dst[:, t * TB:(t + 1) * TB], in_=self.PS[:, b, :]), reads=[self.PB[b]], writes=dl)
        return f

    def evac_act_all(self, dst, dstB, func, bias=None, scale=None):
        S = self.S
        dl = dstB if isinstance(dstB, list) else [dstB]

        def f(banks):
            for t in range(NTB):
                b = banks[t]
                kw = {}
                if bias is not None:
                    kw["bias"] = bias
                    kw["scale"] = scale
                S.op("act", I("activation", out=dst[:, t * TB:(t + 1) * TB], in_=self.PS[:, b, :], func=func, **kw),
                     reads=[self.PB[b], self.derB, self.parB], writes=dl)
        return f

    def rec(self, layer, gnext):
        S = self.S
        PS = self.PS
        PB = self.PB
        r = layer // 2
        one, mone, zero = self.dcol(4), self.dcol(5), self.dcol(9)
        FT = lambda off: self.carve(off, 2048)
        lam = self.par[:, PAR["lam"] + r * 8:PAR["lam"] + r * 8 + 8]
        KD = self.der[:, 16:24]
        S.op("act", I("activation", out=KD, in_=lam, func=AF.Exp, bias=zero, scale=mone), reads=[self.parB, self.derB], writes=[self.derB])
        S.op("act", I("activation", out=KD, in_=KD, func=AF.Ln, bias=one, scale=one), reads=[self.derB], writes=[self.derB])
        S.op("dve", I("tensor_scalar", out=KD, in0=KD, scalar1=-8.0, scalar2=None, op0=ALU.mult), reads=[self.derB], writes=[self.derB])
        LB = self.der[:, 24:32]
        LNOML = self.der[:, 32:40]
        TMP = self.der[:, 40:48]
        TMP2 = self.der[:, 48:56]
        for d in range(2):
            l0 = self.par[:, PAR["lbl"] + (d * 2 + 0) * 4:PAR["lbl"] + (d * 2 + 0) * 4 + 4]
            l1 = self.par[:, PAR["lbl"] + (d * 2 + 1) * 4:PAR["lbl"] + (d * 2 + 1) * 4 + 4]
            t = TMP[:, d * 4:d * 4 + 4]
            t2 = TMP2[:, d * 4:d * 4 + 4]
            lb = LB[:, d * 4:d * 4 + 4]
            S.op("dve", I("tensor_tensor", out=t, in0=l1, in1=l0, op=ALU.subtract), reads=[self.parB, self.derB], writes=[self.derB])
            S.op("act", I("activation", out=t2, in_=t, func=AF.Exp, bias=zero, scale=mone), reads=[self.derB], writes=[self.derB])
            S.op("act", I("activation", out=t, in_=t, func=AF.Exp), reads=[self.derB], writes=[self.derB])
            S.op("dve", I("tensor_scalar", out=t, in0=t, scalar1=1.0, scalar2=None, op0=ALU.add), reads=[self.derB], writes=[self.derB])
            S.op("dve", I("reciprocal", out=t, in_=t), reads=[self.derB], writes=[self.derB])
            S.op("dve", I("tensor_scalar", out=t2, in0=t2, scalar1=1.0, scalar2=None, op0=ALU.add), reads=[self.derB], writes=[self.derB])
            S.op("dve", I("reciprocal", out=t2, in_=t2), reads=[self.derB], writes=[self.derB])
            if r == 0:
                S.op("dve", I("tensor_tensor", out=lb, in0=t, in1=t, op=ALU.subtract), reads=[self.derB], writes=[self.derB])
            else:
                S.op("dve", I("tensor_tensor", out=lb, in0=t, in1=t2, op=ALU.add), reads=[self.derB], writes=[self.derB])
                S.op("dve", I("tensor_tensor", out=lb, in0=lb, in1=t, op=ALU.subtract), reads=[self.derB], writes=[self.derB])
        S.op("act", I("activation", out=LNOML, in_=LB, func=AF.Ln, bias=one, scale=mone), reads=[self.derB], writes=[self.derB])

        P_ = self.carve(0, 2052)
        PB_ = Buf()
        C_ = FT(2052)
        CB_ = Buf()
        Q_ = FT(4100)
        QB_ = Buf()
        R_ = FT(6148)
        RB_ = Buf()
        H_ = FT(8196)
        HB_ = Buf()
        Y_ = FT(10244)
        YB_ = Buf()
        XCB = self.carve(12292, 1024, BF16)
        XCBB = Buf()
        YO = self.carve(13316, 1024, BF16)
        YOB = Buf()
        T1 = P_[:, 0:2048]
        T1b = FT(14340)
        Qb = FT(16388)
        Rb = FT(18436)
        PB2_ = Buf()
        PBL = [PB_, PB2_]
        T1Bp = [[PB_, PB2_], [Buf(), Buf()]]
        QBp = [[Buf(), Buf()], [Buf(), Buf()]]
        RBp = [[Buf(), Buf()], [Buf(), Buf()]]
        YBp = [Buf(), Buf()]
        HBp = [Buf(), Buf()]
        yv = self.d_yscr.rearrange("(c p) t -> p c t", p=128)
        S.op("dve", I("memset", ap=P_[:, 0:2], constant=0.0), writes=PBL)
        S.op("dve", I("memset", ap=P_[:, 2050:2052], constant=0.0), writes=PBL)
        for j in range(4):
            cw = lambda k: self.pcol("conv_w", (r * 4 + k) * 4 + j)
            cb = self.pcol("conv_b", r * 4 + j)
            if j > 0:
                S.op("dve", I("memset", ap=P_[:, 0:2], constant=0.0), writes=PBL)
            if j == 0:
                self.gemm_all(self.evac_copy_all(P_[:, 2:2050], PBL))
            else:
                xa_ev()
            S.op("act", I("activation", out=C_, in_=P_[:, 0:2048], func=AF.Identity, bias=cb, scale=cw(0)), reads=PBL + [self.parB], writes=[CB_])
            for k in range(1, 4):
                S.op("dve", I("scalar_tensor_tensor", out=C_, in0=P_[:, k:k + 2048], scalar=cw(k), in1=C_, op0=ALU.mult, op1=ALU.add),
                     reads=PBL + [CB_, self.parB], writes=[CB_])
            S.op("act", I("copy", out=XCB, in_=C_), reads=[CB_], writes=[XCBB])
            gsl, gwb = self.wq_get()

            def chain(d):
                kd = self.dcol(16 + d * 4 + j)
                NP_ = 2
                L = 2048 // NP_
                T1d, T1B = (T1, T1Bp[0]) if d == 0 else (T1b, T1Bp[1])
                Qd, QdB = (Q_, QBp[0]) if d == 0 else (Qb, QBp[1])
                Rd, RdB = (R_, RBp[0]) if d == 0 else (Rb, RBp[1])
                Td, TdB = (Y_, YBp) if d == 0 else (H_, HBp)
                for gi, (dst, dstB, bname) in enumerate(((T1d, T1B, "b_r"), (Qd, QdB, "b_i"))):
                    m = d * 2 + gi
                    pset = self.gcount % 2
                    self.gcount += 1
                    banks = [pset * 4 + i for i in range(4)]
                    bcol = self.pcol(bname, (r * 2 + d) * 4 + j)
                    for t in range(NTB):
                        S.op("pe", I("matmul", out=PS[:, banks[t], :], lhsT=gsl[:, m * 128:(m + 1) * 128], rhs=XCB[:, t * TB:(t + 1) * TB],
                                     start=True, stop=True), reads=[gwb, XCBB], writes=[PB[banks[t]]])
                    for t in range(NTB):
                        S.op("act", I("activation", out=dst[:, t * TB:(t + 1) * TB], in_=PS[:, banks[t], :], func=AF.Sigmoid, bias=bcol, scale=one),
                             reads=[PB[banks[t]], self.derB, self.parB], writes=[dstB[t * TB // L]])
                    yield
                order = range(NP_) if d == 0 else range(NP_ - 1, -1, -1)
                for p in order:
                    c_ = slice(p * L, (p + 1) * L)
                    S.op("act", I("activation", out=Rd[:, c_], in_=T1d[:, c_], func=AF.Tanh, bias=zero, scale=kd), reads=[T1B[p], self.derB], writes=[RdB[p]])
                yield
                for p in order:
                    c_ = slice(p * L, (p + 1) * L)
                    S.op("act", I("activation", out=T1d[:, c_], in_=T1d[:, c_], func=AF.Exp, bias=zero, scale=kd), reads=[T1B[p], self.derB], writes=[T1B[p]])
                yield
                for p in order:
                    c_ = slice(p * L, (p + 1) * L)
                    S.op("dve", I("tensor_tensor", out=Td[:, c_], in0=T1d[:, c_], in1=T1d[:, c_], op=ALU.mult), reads=[T1B[p]], writes=[TdB[p]])
                yield
                for p in order:
                    c_ = slice(p * L, (p + 1) * L)
                    S.op("dve", I("scalar_tensor_tensor", out=Rd[:, c_], in0=Td[:, c_], scalar=1.0, in1=Rd[:, c_], op0=ALU.add, op1=ALU.mult),
                         reads=[TdB[p], RdB[p]], writes=[RdB[p]])
                yield
                for p in order:
                    c_ = slice(p * L, (p + 1) * L)
                    S.op("act", I("activation", out=Rd[:, c_], in_=Rd[:, c_], func=AF.Sqrt, bias=zero, scale=mone), reads=[RdB[p], self.derB], writes=[RdB[p]])
                yield
                for p in order:
                    c_ = slice(p * L, (p + 1) * L)
                    S.op("dve", I("tensor_tensor", out=Qd[:, c_], in0=Qd[:, c_], in1=C_[:, c_], op=ALU.mult), reads=[QdB[p], CB_], writes=[QdB[p]])
                yield
                for p in order:
                    c_ = slice(p * L, (p + 1) * L)
                    S.op("dve", I("tensor_tensor", out=Qd[:, c_], in0=Qd[:, c_], in1=Rd[:, c_], op=ALU.mult), reads=[QdB[p], RdB[p]], writes=[QdB[p]])
                yield
                for p in order:
                    c_ = slice(p * L, (p + 1) * L)
                    if d == 0:
                        init = 0.0 if p == 0 else Y_[:, p * L - 1:p * L]
                        S.op("dve", I("tensor_tensor_scan", out=Y_[:, c_], data0=T1d[:, c_], data1=Qd[:, c_], initial=init, op0=ALU.mult, op1=ALU.add),
                             reads=[T1B[p], QdB[p]] + ([YBp[p - 1]] if p > 0 else []), writes=[YBp[p]])
                    else:
                        init = 0.0 if p == NP_ - 1 else H_[:, (p + 1) * L:(p + 1) * L + 1]
                        S.op("dve", I("tensor_tensor_scan", out=H_[:, c_][:, ::-1], data0=T1d[:, c_][:, ::-1], data1=Qd[:, c_][:, ::-1], initial=init,
                                      op0=ALU.mult, op1=ALU.add),
                             reads=[T1B[p], QdB[p]] + ([HBp[p + 1]] if p < NP_ - 1 else []), writes=[HBp[p]])
                yield
            gens = [chain(0), chain(1)]
            for _ in range(2):
                for gobj in gens:
                    next(gobj)
            ga_ev, _ = self.gemm_all(self.evac_act_all(R_, RBp[0], AF.Gelu), defer_evac=True)
            if j < 3:
                xa_ev, _ = self.gemm_all(self.evac_copy_all(P_[:, 2:2050], PBL), defer_evac=True)
            while gens:
                for gobj in list(gens):
                    try:
                        next(gobj)
                    except StopIteration:
                        gens.remove(gobj)
            S.op("dve", I("tensor_tensor", out=Y_, in0=Y_, in1=H_, op=ALU.add), reads=YBp + HBp, writes=YBp)
            ga_ev()
            S.op("dve", I("tensor_tensor", out=YO, in0=Y_, in1=R_, op=ALU.mult), reads=YBp + RBp[0], writes=[YOB] + RBp[0])
            S.dma("sp", I("dma_start", out=yv[:, j, :], in_=YO), reads=[YOB], chan="ysc")
        S.barrier()

        QF = FT(0)
        QFB = Buf()
        Z = FT(2048)
        ZB = [Buf(), Buf()]
        A = FT(4096)
        AB = [Buf(), Buf()]
        BT = FT(6144)
        BTB = [Buf(), Buf()]
        OT = FT(8192)
        OTB = [Buf() for _ in range(16)]
        QE = [self.carve(10240 + i * 1024, 1024, BF16) for i in range(2)]
        QEB = [Buf(), Buf()]
        KE = [self.carve(12288 + i * 1024, 1024, BF16) for i in range(2)]
        KEB = [Buf(), Buf()]
        KDT = self.carve(14336, 1024, BF16)
        KDTB = [Buf(), Buf()]
        KDTOK = [self.carve(15360 + i * 1024, 1024, BF16).rearrange("p (t d) -> p t d", t=16) for i in range(2)]
        KDTOKB = [Buf(), Buf()]
        VTOK = self.carve(17408, 1024, BF16).rearrange("p (t d) -> p t d", t=16)
        VTOKB = Buf()
        SBF = self.carve(18432, 2112, BF16).rearrange("p (n v) -> p n v", n=33)
        SBFB = [Buf() for _ in range(33)]
        SF = self.carve(20544, 256).rearrange("p (b v) -> p b v", b=2)
        SFB = [Buf(), Buf()]
        ATTB = self.carve(20800, 128, BF16).rearrange("p (b c) -> p b c", b=2)
        ATTBB = [Buf(), Buf()]
        MASKC = self.carve(20928, 1024, BF16)
        MASKB = Buf()
        DEC = [self.carve(21952 + i * 32, 32) for i in range(2)]
        EMID = [self.carve(22016 + i * 32, 32) for i in range(2)]
        DECB = [Buf(), Buf()]
        BL = self.carve(22080, 32)
        BM = self.carve(22112, 32)
        BLB = [Buf(), Buf()]
        SBFd = [SBF, self.carve(2048, 2112, BF16).rearrange("p (n v) -> p n v", n=33)]
        SBFBd = [SBFB, [Buf() for _ in range(33)]]
        SFd = [SF, self.carve(4224, 256).rearrange("p (b v) -> p b v", b=2)]
        SFBd = [SFB, [Buf(), Buf()]]
        ATTBd = [ATTB, self.carve(4480, 128, BF16).rearrange("p (b c) -> p b c", b=2)]
        ATTBBd = [ATTBB, [Buf(), Buf()]]
        OT2 = BT
        OUTd = [OT, OT2]
        OUTBd = [OTB, [Buf() for _ in range(16)]]
        G = Z
        RSH = A[:, 0:512]
        SQh2 = [self.carve(6144 + i * 256, 256, BF16) for i in range(2)]
        SQhB2 = [Buf(), Buf()]
        RSH2 = [A[:, i * 512:(i + 1) * 512] for i in range(2)]
        RSHB2 = [Buf(), Buf()]
        S.op("dve", I("tensor_copy", out=MASKC.rearrange("p (n c) -> p n c", c=64),
                      in_=self.cst[:, CST["cmask"]:CST["cmask"] + 64].rearrange("p (o c) -> p o c", o=1).to_broadcast([128, 32, 64])),
             reads=[self.cstB], writes=[MASKB])
        S.op("dve", I("memset", ap=SBF[:, 0, :], constant=0.0), writes=[SBFB[0]])
        ident = self.cmat("ident")
        ch = lambda ap: ap.rearrange("p (n c) -> p n c", c=64)
        bc = lambda ap: ap.rearrange("p (n o) -> p n o", o=1).to_broadcast([128, 32, 64])

        def prep(j, d, zev, zset):
            lbc = self.dcol(24 + d * 4 + j)
            lno = self.dcol(32 + d * 4 + j)
            rv = (lambda ap: ap) if d == 0 else (lambda ap: ap[:, ::-1])
            pmid = 31 if d == 0 else 32
            plast = 63 if d == 0 else 0
            NP_ = 2
            L = 2048 // NP_
            NC_ = 32 // NP_
            cs = lambda p: slice(p * L, (p + 1) * L)
            ns = lambda p: slice(p * NC_, (p + 1) * NC_)
            bcp = lambda ap: ap.rearrange("p (n o) -> p n o", o=1).to_broadcast([128, NC_, 64])
            zev()
            yield
            for p in range(NP_):
                S.op("act", I("activation", out=A[:, cs(p)], in_=Z[:, cs(p)], func=AF.Exp, bias=zero, scale=mone), reads=[ZB[p], self.derB], writes=[AB[p]])
            yield
            for p in range(NP_):
                S.op("act", I("activation", out=BT[:, cs(p)], in_=A[:, cs(p)], func=AF.Ln, bias=one, scale=one), reads=[AB[p], self.derB], writes=[BTB[p]])
            yield
            for p in range(NP_):
                S.op("act", I("activation", out=A[:, cs(p)], in_=A[:, cs(p)], func=AF.Ln, bias=one, scale=lbc), reads=[AB[p], self.derB], writes=[AB[p]])
            yield
            for p in range(NP_):
                S.op("dve", I("tensor_tensor", out=A[:, cs(p)], in0=A[:, cs(p)], in1=BT[:, cs(p)], op=ALU.subtract), reads=[AB[p], BTB[p]], writes=[AB[p]])
            yield
            for p in range(NP_):
                S.op("dve", I("scalar_tensor_tensor", out=Z[:, cs(p)], in0=Z[:, cs(p)], scalar=-1.0, in1=BT[:, cs(p)], op0=ALU.mult, op1=ALU.subtract),
                     reads=[ZB[p], BTB[p]], writes=[ZB[p]])
            yield
            for p in range(NP_):
                S.op("dve", I("tensor_tensor_scan", out=rv(BT[:, cs(p)]), data0=MASKC[:, 0:L], data1=rv(A[:, cs(p)]), initial=0.0,
                              op0=ALU.mult, op1=ALU.add), reads=[MASKB, AB[p]], writes=[BTB[p]])
            yield
            for p in range(NP_):
                S.op("act", I("copy", out=BL[:, ns(p)], in_=ch(BT[:, cs(p)])[:, :, plast]), reads=[BTB[p]], writes=[BLB[p]])
                S.op("act", I("copy", out=BM[:, ns(p)], in_=ch(BT[:, cs(p)])[:, :, pmid]), reads=[BTB[p]], writes=[BLB[p]])
                S.op("act", I("activation", out=DEC[d][:, ns(p)], in_=BL[:, ns(p)], func=AF.Exp), reads=[BLB[p]], writes=[DECB[d]])
                S.op("act", I("activation", out=EMID[d][:, ns(p)], in_=BM[:, ns(p)], func=AF.Exp), reads=[BLB[p]], writes=[DECB[d]])
            yield
            for p in range(NP_):
                S.op("dve", I("tensor_tensor", out=ch(A[:, cs(p)]), in0=ch(BT[:, cs(p)]), in1=bcp(BM[:, ns(p)]), op=ALU.subtract),
                     reads=[BTB[p], BLB[p], AB[p]], writes=[AB[p]])
            yield
            for p in range(NP_):
                S.op("dve", I("tensor_tensor", out=ch(BT[:, cs(p)]), in0=bcp(BL[:, ns(p)]), in1=ch(BT[:, cs(p)]), op=ALU.subtract),
                     reads=[BTB[p], BLB[p]], writes=[BTB[p]])
            yield
            for p in range(NP_):
                S.op("dve", I("tensor_tensor", out=BT[:, cs(p)], in0=BT[:, cs(p)], in1=Z[:, cs(p)], op=ALU.add), reads=[BTB[p], ZB[p]], writes=[BTB[p]])
            yield
            for p in range(NP_):
                S.op("act", I("activation", out=KDT[:, cs(p)], in_=BT[:, cs(p)], func=AF.Exp, bias=lno, scale=one), reads=[BTB[p], self.derB], writes=[KDTB[p]])
            yield
            for p in range(NP_):
                S.op("dve", I("tensor_tensor", out=BT[:, cs(p)], in0=Z[:, cs(p)], in1=A[:, cs(p)], op=ALU.subtract), reads=[ZB[p], AB[p], BTB[p]], writes=[BTB[p]])
            yield
            for p in range(NP_):
                S.op("act", I("activation", out=KE[d][:, cs(p)], in_=BT[:, cs(p)], func=AF.Exp, bias=lno, scale=one), reads=[BTB[p], self.derB], writes=[KEB[d]])
            yield
            for p in range(NP_):
                S.op("act", I("activation", out=A[:, cs(p)], in_=A[:, cs(p)], func=AF.Exp), reads=[AB[p]], writes=[AB[p]])
            yield
            for p in range(NP_):
                S.op("dve", I("tensor_tensor", out=QE[d][:, cs(p)], in0=A[:, cs(p)], in1=QF[:, cs(p)], op=ALU.mult), reads=[AB[p], QFB], writes=[QEB[d]])
            yield
            for q4 in range(4):
                b = 4 * zset + q4
                pb16 = PS[:, b, :].bitcast(BF16)
                for i in range(4):
                    tt = q4 * 4 + i
                    S.op("pe", I("transpose", out=pb16[:, i * 128:(i + 1) * 128], in_=KDT[:, tt * 128:(tt + 1) * 128], identity=ident),
                         reads=[KDTB[tt * 128 // L], self.cstB], writes=[PB[b]], inc=(i == 3))
                S.op("act", I("copy", out=KDTOK[d][:, q4 * 4:(q4 + 1) * 4, :], in_=pb16[:, 0:512].rearrange("p (t d) -> p t d", t=4)),
                     reads=[PB[b]], writes=[KDTOKB[d]])
                yield

        def passes(j, d):
            hm = self.cmat("hmf" if d == 0 else "hmb")
            SBF_, SBFB_, SF_, SFB_, ATT_, ATTB_, OUT_, OUTB_ = SBFd[d], SBFBd[d], SFd[d], SFBd[d], ATTBd[d], ATTBBd[d], OUTd[d], OUTBd[d]
            bb = 4 * d
            S.op("dve", I("memset", ap=SF_[:, 0, :], constant=0.0), writes=[SFB_[0]])
            S.op("dve", I("memset", ap=SBF_[:, 0, :], constant=0.0), writes=[SBFB_[0]])

            def tile_out(k):
                tt = k if d == 0 else 15 - k
                ab = k % 2
                b = bb + 2 + (k % 2)
                tsl = slice(tt * 128, (tt + 1) * 128)
                S.op("pe", I("matmul", out=PS[:, b, 0:128], lhsT=KE[d][:, tsl], rhs=QE[d][:, tsl], start=True, stop=True),
                     reads=[KEB[d], QEB[d]], writes=[PB[b]])
                S.op("dve", I("tensor_tensor", out=ATT_[:, ab, :], in0=PS[:, b, 0:128], in1=hm, op=ALU.mult),
                     reads=[PB[b], self.cstB], writes=[ATTB_[ab]])
                S.op("pe", I("matmul", out=PS[:, b, 128:256], lhsT=VTOK[:, tt, :], rhs=ATT_[:, ab, :], start=True, stop=False,
                             skip_group_check=True),
                     reads=[VTOKB, ATTB_[ab]], writes=[PB[b]], inc=False)
                cis = (2 * tt, 2 * tt + 1)
                for ci in cis:
                    n = ci if d == 0 else 31 - ci
                    S.op("pe", I("matmul", out=PS[:, b, 128 + (ci % 2) * 64:128 + (ci % 2) * 64 + 64], lhsT=SBF_[:, n, :],
                                 rhs=QE[d][:, ci * 64:ci * 64 + 64], start=False, stop=(ci == cis[-1]), skip_group_check=True),
                         reads=[SBFB_[n], QEB[d]], writes=[PB[b]], inc=(ci == cis[-1]))
                S.op("act", I("copy", out=OUT_[:, tsl], in_=PS[:, b, 128:256]), reads=[PB[b]], writes=[OUTB_[tt]])
            for n in range(31):
                cn = n if d == 0 else 31 - n
                tt, pr = cn // 2, slice((cn % 2) * 64, (cn % 2) * 64 + 64)
                b = bb + (n % 2)
                col = 0
                S.op("pe", I("matmul", out=PS[:, b, col:col + 128], lhsT=KDTOK[d][pr, tt, :], rhs=VTOK[pr, tt, :], start=True, stop=True),
                     reads=[KDTOKB[d], VTOKB], writes=[PB[b]])
                cur, nxt = n % 2, (n + 1) % 2
                S.op("dve", I("scalar_tensor_tensor", out=SF_[:, nxt, :], in0=SF_[:, cur, :], scalar=DEC[d][:, cn:cn + 1],
                              in1=PS[:, b, col:col + 128], op0=ALU.mult, op1=ALU.add),
                     reads=[SFB_[cur], DECB[d], PB[b]], writes=[SFB_[nxt]])
                cnn = (n + 1) if d == 0 else 31 - (n + 1)
                S.op("act", I("activation", out=SBF_[:, n + 1, :], in_=SF_[:, nxt, :], func=AF.Identity, bias=zero, scale=EMID[d][:, cnn:cnn + 1]),
                     reads=[SFB_[nxt], DECB[d], self.derB], writes=[SBFB_[n + 1]])
                yield
                if n % 2 == 0:
                    tile_out(n // 2)
                    yield

        def run_interleaved(gens_w):
            gens_w = [[g, w] for g, w in gens_w]
            while gens_w:
                for item in list(gens_w):
                    g, w = item
                    for _ in range(w):
                        try:
                            next(g)
                        except StopIteration:
                            gens_w.remove(item)
                            break

        def vq_gemms():
            pset = self.gcount % 2
            self.gcount += 1
            banks = [pset * 4 + i for i in range(4)]
            sl, wb = self.wq_get()
            slv = sl.rearrange("p (k c) -> p k c", k=8)
            for tt in range(16):
                b = banks[tt // 4]
                col = (tt % 4) * 128
                for kc in range(8):
                    S.op("pe", I("matmul", out=PS[:, b, col:col + 128], lhsT=self.XT[:, kc, tt * 128:(tt + 1) * 128],
                                 rhs=slv[:, kc, :], start=(kc == 0), stop=(kc == 7)),
                         reads=[wb, self.XB[kc][tt // 4]], writes=[PB[b]], inc=(kc == 7 and tt % 4 == 3))
            for bi in range(4):
                S.op("act", I("copy", out=VTOK[:, bi * 4:(bi + 1) * 4, :], in_=PS[:, banks[bi], :].rearrange("p (t d) -> p t d", t=4)),
                     reads=[PB[banks[bi]]], writes=[VTOKB])
            self.gemm_all(self.evac_copy_all(QF, QFB, "act"))

        vq_gemms()
        for j in range(4):
            zev0, zset0 = self.gemm_all(self.evac_copy_all(Z, ZB, "dve"), defer_evac=True)
            zev1, zset1 = self.gemm_all(self.evac_copy_all(Z, ZB, "dve"), defer_evac=True)
            run_interleaved([(prep(j, 0, zev0, zset0), 1)])
            run_interleaved([(prep(j, 1, zev1, zset1), 1)])
            S.barrier()
            run_interleaved([(passes(j, 0), 1), (passes(j, 1), 1)])
            S.barrier()
            for t in range(NTB):
                ts = slice(t * TB, (t + 1) * TB)
                S.op("dve", I("tensor_tensor", out=OT[:, ts], in0=OT[:, ts], in1=OT2[:, ts], op=ALU.add),
                     reads=OTB[t * 4:(t + 1) * 4] + OUTBd[1][t * 4:(t + 1) * 4], writes=OTB[t * 4:(t + 1) * 4])
            if j < 3:
                vq_gemms()
            self.gemm_all(self.evac_act_all(G, ZB, AF.Silu))
            hcol = self.pcol("hng", r * 4 + j)
            for t in range(NTB):
                ts = slice(t * TB, (t + 1) * TB)
                k2 = t % 2
                sqh = SQh2[k2]
                rsh = RSH2[k2]
                S.op("act", I("activation", out=sqh, in_=OT[:, ts], func=AF.Square), reads=OTB[t * 4:(t + 1) * 4] + BTB, writes=[SQhB2[k2]] + BTB)
                b = (self.gcount % 2) * 4 + k2
                S.op("pe", I("matmul", out=PS[:, b, :], lhsT=self.cmat("ones"), rhs=sqh, start=True, stop=True),
                     reads=[self.cstB, SQhB2[k2]], writes=[PB[b]])
                S.op("act", I("activation", out=rsh, in_=PS[:, b, :], func=AF.Ln, bias=self.dcol(0), scale=self.dcol(2)),
                     reads=[PB[b], self.derB] + AB, writes=[RSHB2[k2]] + AB)
                S.op("act", I("activation", out=rsh, in_=rsh, func=AF.Exp, scale=-0.5), reads=[RSHB2[k2]], writes=[RSHB2[k2]])
                S.op("dve", I("scalar_tensor_tensor", out=OT[:, ts], in0=OT[:, ts], scalar=hcol, in1=rsh, op0=ALU.mult, op1=ALU.mult),
                     reads=OTB[t * 4:(t + 1) * 4] + [RSHB2[k2], self.parB], writes=OTB[t * 4:(t + 1) * 4])
                S.op("dve", I("tensor_tensor", out=QE[0][:, ts], in0=OT[:, ts], in1=G[:, ts], op=ALU.mult),
                     reads=OTB[t * 4:(t + 1) * 4] + ZB + [QEB[0]], writes=[QEB[0]])
            self.gcount += 1
            S.dma("sp", I("dma_start", out=yv[:, 4 + j, :], in_=QE[0]), reads=[QEB[0]], chan="ysc")
        S.barrier()
        for c in range(NCH):
            S.dma("sp", I("dma_start", out=self.XT[:, c, :], in_=yv[:, c, :]), reads=[QEB[0], YOB], writes=self.XB[c], chan="yld%d" % c)
        r_ = r
        self.out_proj(layer, gnext)

    def plan_att(self, layer):
        a = layer // 2
        W = self.d_w["att_w_qkv"][a]
        items = []
        for oc in range(8):
            items += self.piece_cols(W, oc * 128)
        for g in range(4):
            src = W[:, 1024 + g * 64:1024 + (g + 1) * 64].rearrange("(k p) c -> p k c", p=128)
            items.append([(lambda sl: sl.rearrange("p (k c) -> p k c", k=8)[:, :, 0:64], src),
                          (lambda sl: sl.rearrange("p (k c) -> p k c", k=8)[:, :, 64:128], src)])
        for vh in range(2):
            items += self.piece_cols(W, 1280 + vh * 128)
        Wo = self.d_w["att_w_o"][a]
        for tb in range(NTB):
            for g in range(2):
                items += self.pieces_tb(Wo, 8, g * 512)
        self.wq_plan(items)

    def att(self, layer, gnext):
        S = self.S
        PS = self.PS
        PB = self.PB
        a = layer // 2
        QR = self.carve(0, 12288, BF16).rearrange("p (c t) -> p c t", c=12)
        QRB = [[Buf() for _ in range(NTB)] for _ in range(12)]
        VD = self.carve(12288, 4096, BF16).rearrange("p (t g d) -> p t g d", t=16, g=4)
        VDB = [Buf() for _ in range(16)]
        CS = self.carve(16384, 4096).rearrange("p (s t) -> p s t", s=2)
        CSB = Buf()
        QFs = [self.carve(12288 + i * 512, 512) for i in range(4)]
        QF2s = [self.carve(14336 + i * 512, 512) for i in range(4)]
        QBs = [self.carve(20480 + i * 256, 256, BF16) for i in range(4)]
        QFB = [Buf() for _ in range(4)]
        QF2B = [Buf() for _ in range(4)]
        QBB = [Buf() for _ in range(4)]
        SQHs = [self.carve(20480 + i * 256, 256, BF16) for i in range(4)]
        SQHB = [Buf() for _ in range(4)]
        STAT = self.carve(22016, 48).rearrange("p (h t) -> p h t", h=12)
        STATB = Buf()
        M2 = self.carve(22064, 12)
        M2b = self.carve(22076, 8, BF16)
        MS = self.carve(22084, 32)
        PROD = self.carve(22116, 16)
        NEGC = self.carve(22132, 16)
        ESK = self.carve(22148, 16)
        ESP = self.carve(22164, 8)
        SMB = Buf()
        PT = self.carve(16384, 768, BF16).rearrange("p (b h j q) -> p b h j q", b=2, h=2, j=3)
        PTB = [[Buf(), Buf()], [Buf(), Buf()]]
        RD = self.carve(17152, 256).rearrange("p (b q) -> p b q", b=2)
        RDB = [Buf(), Buf()]
        S.dma("sp", I("dma_start", out=CS[:, 0, :], in_=self.d_cos[:, :]), writes=[CSB])
        S.dma("sp", I("dma_start", out=CS[:, 1, :], in_=self.d_sin[:, :]), writes=[CSB])
        rot = self.cmat("rot")
        ND1 = 0
        for oc in range(12):
            def evac(banks, oc=oc):
                for t in range(NTB):
                    b = banks[t]
                    ts = slice(t * TB, (t + 1) * TB)
                    S.op("act", I("copy", out=QBs[t], in_=PS[:, b, :]), reads=[PB[b]], writes=[QBB[t]])
                    S.op("dve", I("tensor_tensor", out=QFs[t], in0=PS[:, b, :], in1=CS[:, 0, ts], op=ALU.mult),
                         reads=[PB[b], CSB], writes=[QFB[t]])
                for t in range(NTB):
                    b = banks[t]
                    S.op("pe", I("matmul", out=PS[:, b, :], lhsT=rot, rhs=QBs[t], start=True, stop=True),
                         reads=[self.cstB, QBB[t]], writes=[PB[b]])
                for t in range(NTB):
                    b = banks[t]
                    ts = slice(t * TB, (t + 1) * TB)
                    S.op("dve", I("tensor_tensor", out=QF2s[t], in0=PS[:, b, :], in1=CS[:, 1, ts], op=ALU.mult),
                         reads=[PB[b], CSB], writes=[QF2B[t]])
                    S.op("pool", I("tensor_tensor", out=QR[:, oc, ts], in0=QFs[t], in1=QF2s[t], op=ALU.add),
                         reads=[QFB[t], QF2B[t]], writes=[QRB[oc][t]])
            self.gemm_all(evac)
            for _ in range(ND1):
                bdm = (self.gcount % 2) * 4 + 3
                S.op("pe", I("matmul", out=PS[:, bdm, :], lhsT=rot, rhs=self.XT[:, 0, 0:512], start=True, stop=True),
                     reads=[self.cstB], writes=[PB[bdm]], inc=False)
        S.barrier()
        for vh in range(2):
            pset = self.gcount % 2
            self.gcount += 1
            banks = [pset * 4 + i for i in range(4)]
            sl, wb = self.wq_get()
            slv = sl.rearrange("p (k c) -> p k c", k=8)
            for tt in range(16):
                b = banks[tt // 4]
                col = (tt % 4) * 128
                for kc in range(8):
                    last = (kc == 7 and tt % 4 == 3)
                    S.op("pe", I("matmul", out=PS[:, b, col:col + 128], lhsT=self.XT[:, kc, tt * 128:(tt + 1) * 128],
                                 rhs=slv[:, kc, :], start=(kc == 0), stop=(kc == 7)),
                         reads=[wb, self.XB[kc][tt // 4]], writes=[PB[b]], inc=last)
            for bi in range(4):
                b = banks[bi]
                src = PS[:, b, :].rearrange("p (t g d) -> p t g d", t=4, g=2)
                wr = [VDB[bi * 4 + i] for i in range(4)]
                dst0 = VD[:, bi * 4:(bi + 1) * 4, vh * 2:vh * 2 + 2, 0:64]
                dst1 = VD[:, bi * 4:(bi + 1) * 4, vh * 2:vh * 2 + 2, 64:128]
                S.op("act", I("copy", out=dst0, in_=src), reads=[PB[b]], writes=wr)
                S.op("dve", I("tensor_copy", out=dst1, in_=src), reads=[PB[b]], writes=wr)
        bd = self.cmat("bdiag")
        for oc in range(12):
            for t in range(NTB):
                ts = slice(t * TB, (t + 1) * TB)
                S.op("act", I("activation", out=SQHs[t], in_=QR[:, oc, ts], func=AF.Square), reads=[QRB[oc][t]], writes=[SQHB[t]])
                b = (self.gcount % 8)
                self.gcount += 1
                S.op("pe", I("matmul", out=PS[:, b, :], lhsT=bd, rhs=SQHs[t], start=True, stop=True),
                     reads=[self.cstB, SQHB[t]], writes=[PB[b]])
                S.op("dve", I("reduce_max", out=STAT[:, oc, t:t + 1], in_=PS[:, b, :], axis=AX.X), reads=[PB[b]], writes=[STATB])
        S.op("dve", I("reduce_max", out=M2, in_=STAT, axis=AX.X), reads=[STATB], writes=[SMB])
        S.op("dve", I("tensor_copy", out=M2b[:, 0:12], in_=M2), reads=[SMB], writes=[SMB])
        b = self.gcount % 8
        self.gcount += 1
        S.op("pe", I("matmul", out=PS[:, b, 0:12], lhsT=self.cmat("e0"), rhs=M2b[:, 0:12], start=True, stop=True),
             reads=[self.cstB, SMB], writes=[PB[b]])
        S.op("pe", I("matmul", out=PS[:, b, 16:28], lhsT=self.cmat("e1"), rhs=M2b[:, 0:12], start=True, stop=True),
             reads=[self.cstB, SMB], writes=[PB[b]])
        S.op("dve", I("tensor_copy", out=MS, in_=PS[:, b, 0:32]), reads=[PB[b]], writes=[SMB])
        PRODv = PROD.rearrange("p (g o two) -> p g o two", g=4, o=2, two=2)
        KBC = MS[:, 8:12].rearrange("p (g o) -> p g o", o=1).to_broadcast([128, 4, 2])
        S.op("dve", I("tensor_tensor", out=PRODv[:, :, :, 0], in0=MS[:, 0:8].rearrange("p (g o) -> p g o", o=2), in1=KBC, op=ALU.mult),
             reads=[SMB], writes=[SMB])
        S.op("dve", I("tensor_tensor", out=PRODv[:, :, :, 1], in0=MS[:, 16:24].rearrange("p (g o) -> p g o", o=2), in1=KBC, op=ALU.mult),
             reads=[SMB], writes=[SMB])
        S.op("act", I("activation", out=PROD, in_=PROD, func=AF.Ln), reads=[SMB], writes=[SMB])
        S.op("act", I("activation", out=PROD, in_=PROD, func=AF.Exp, scale=0.5), reads=[SMB], writes=[SMB])
        S.op("dve", I("tensor_scalar", out=NEGC, in0=PROD, scalar1=-0.125 / 64.0 * 1.01, scalar2=None, op0=ALU.mult), reads=[SMB], writes=[SMB])
        so = PAR["sinks"] + a * 16
        NEGCP = self.carve(22172, 8)
        NEGCv = NEGC.rearrange("p (g two hi) -> p g two hi", g=4, two=2)
        NEGCPv = NEGCP.rearrange("p (g hi) -> p g hi", g=4)
        S.op("dve", I("tensor_tensor", out=NEGCPv, in0=NEGCv[:, :, 0, :], in1=NEGCv[:, :, 1, :], op=ALU.min), reads=[SMB], writes=[SMB])
        S.op("dve", I("tensor_tensor", out=ESK.rearrange("p (g two hi) -> p g two hi", g=4, two=2),
                      in0=self.par[:, so:so + 16].rearrange("p (g two hi) -> p g two hi", g=4, two=2),
                      in1=NEGCPv.rearrange("p g (o hi) -> p g o hi", o=1).to_broadcast([128, 4, 2, 2]), op=ALU.add),
             reads=[SMB, self.parB], writes=[SMB])
        S.op("act", I("activation", out=ESK, in_=ESK, func=AF.Exp), reads=[SMB], writes=[SMB])
        ESKv = ESK.rearrange("p (c two) -> p c two", two=2)
        S.op("dve", I("tensor_copy", out=ESP[0:64, :], in_=ESKv[0:64, :, 0]), reads=[SMB], writes=[SMB])
        S.op("dve", I("tensor_copy", out=ESP[64:128, :], in_=ESKv[64:128, :, 1]), reads=[SMB], writes=[SMB])
        ESPb = self.carve(22180, 4, BF16)
        ESPT = self.carve(22184, 64, BF16)
        S.op("dve", I("tensor_copy", out=ESPb[:, 0:8], in_=ESP), reads=[SMB], writes=[SMB])
        bt_ = self.gcount % 8
        self.gcount += 1
        pbt = PS[:, bt_, :].bitcast(BF16)
        S.op("pe", I("transpose", out=pbt[0:8, 0:128], in_=ESPb[:, 0:8], identity=self.cmat("ident")), reads=[SMB, self.cstB], writes=[PB[bt_]])
        S.op("dve", I("tensor_copy", out=ESPT[0:8, :], in_=pbt[0:8, 0:128]), reads=[PB[bt_]], writes=[SMB])
        S.barrier()
        PT = self.carve(16384, 768, BF16).rearrange("p (b j h q) -> p b j h q", b=2, j=3, h=2)
        PTB = [[Buf(), Buf()], [Buf(), Buf()]]
        RD = self.carve(17152, 512).rearrange("p (b q) -> p b q", b=2)
        RDB = [Buf(), Buf()]
        SEL = self.carve(17664, 512, BF16).rearrange("p (c q) -> p c q", c=8)
        S.op("dve", I("tensor_copy", out=SEL[0:8, :, :],
                      in_=self.cmat("ident")[0:8, 0:8].rearrange("p (c o) -> p c o", o=1).to_broadcast([8, 8, 128])),
             reads=[self.cstB], writes=[SMB])
        nmprev = self.cmat("nmprev").rearrange("p (o q) -> p o q", o=1).to_broadcast([128, 2, 128])
        nmnext = self.cmat("nmnext").rearrange("p (o q) -> p o q", o=1).to_broadcast([128, 2, 128])
        identm = self.cmat("ident")
        onesm = self.cmat("ones")
        OB = [[[Buf() for _ in range(2)] for _ in range(4)] for _ in range(16)]
        it = 0
        NDUMMY = 4
        for n in range(16):
            js = [j for j in (n - 1, n, n + 1) if 0 <= j < 16]
            qs = slice(n * 128, (n + 1) * 128)
            for g in range(4):
                for hi in range(2):
                    sb_ = it % 2
                    it += 1
                    bA, bB, bO = sb_ * 3, sb_ * 3 + 1, sb_ * 3 + 2
                    pr = slice(64 * hi, 64 * hi + 64)
                    pidx = g * 2 + hi
                    qmov = QR[pr, 2 * g:2 * g + 2, qs]
                    qbufs = [QRB[2 * g][n // 4], QRB[2 * g + 1][n // 4]]

                    def sc_out(jj):
                        if jj < 2:
                            return PS[:, bA, jj * 256:(jj + 1) * 256].rearrange("p (h q) -> p h q", h=2), bA
                        return PS[:, bB, 0:256].rearrange("p (h q) -> p h q", h=2), bB
                    for j in js:
                        jj = j - (n - 1)
                        o_ap, bk = sc_out(jj)
                        msk = None if jj == 1 else (nmprev if jj == 0 else nmnext)
                        S.op("pe", I("matmul", out=o_ap, lhsT=QR[pr, 8 + g, j * 128:(j + 1) * 128], rhs=qmov, start=True, stop=(msk is None)),
                             reads=[QRB[8 + g][j // 4]] + qbufs, writes=[PB[bk]], inc=(msk is None))
                        if msk is not None:
                            S.op("pe", I("matmul", out=o_ap, lhsT=identm, rhs=msk, start=False, stop=True),
                                 reads=[self.cstB], writes=[PB[bk]])
                    for _ in range(NDUMMY):
                        S.op("pe", I("matmul", out=PS[:, 6 + (it % 2), :], lhsT=identm, rhs=QR[:, 8, 0:512], start=True, stop=True),
                             reads=[self.cstB], writes=[PB[6 + (it % 2)]], inc=False)
                    pt = PT[:, sb_, :, :, :]
                    bias = NEGCP[:, pidx:pidx + 1]
                    jA = [j - (n - 1) for j in js if j - (n - 1) < 2]
                    if jA:
                        S.op("act", I("activation", out=pt[:, jA[0]:jA[-1] + 1, :, :],
                                      in_=PS[:, bA, jA[0] * 256:(jA[-1] + 1) * 256].rearrange("p (j h q) -> p j h q", j=len(jA), h=2),
                                      func=AF.Exp, bias=bias, scale=self.dcol(3)),
                             reads=[PB[bA], SMB, self.derB], writes=[PTB[sb_][0]])
                    if n + 1 < 16:
                        S.op("act", I("activation", out=pt[:, 2, :, :], in_=PS[:, bB, 0:256].rearrange("p (h q) -> p h q", h=2),
                                      func=AF.Exp, bias=bias, scale=self.dcol(3)),
                             reads=[PB[bB], SMB, self.derB], writes=[PTB[sb_][1]])
                    o_out = PS[:, bO, 0:256].rearrange("p (h q) -> p h q", h=2)
                    d_out = PS[:, bO, 256:512].rearrange("p (h q) -> p h q", h=2)
                    for j in js:
                        jj = j - (n - 1)
                        S.op("pe", I("matmul", out=o_out, lhsT=VD[:, j, g, :], rhs=pt[:, jj, :, :], start=(j == js[0]), stop=(j == js[-1])),
                             reads=[VDB[j], PTB[sb_][jj // 2]], writes=[PB[bO]], inc=False)
                    for j in js:
                        jj = j - (n - 1)
                        S.op("pe", I("matmul", out=d_out, lhsT=onesm, rhs=pt[:, jj, :, :], start=(j == js[0]), stop=False),
                             reads=[self.cstB, PTB[sb_][jj // 2]], writes=[PB[bO]], inc=False)
                    S.op("pe", I("matmul", out=d_out, lhsT=ESPT[0:8, :], rhs=SEL[0:8, 2 * g:2 * g + 2, :], start=False, stop=True),
                         reads=[SMB], writes=[PB[bO]])
                    rd = RD[:, sb_, :].rearrange("p (h q) -> p h q", h=2)
                    S.op("dve", I("reciprocal", out=rd[pr], in_=d_out[pr]), reads=[PB[bO]], writes=[RDB[sb_]])
                    S.op("dve", I("tensor_tensor", out=self.XT[pr, 2 * g:2 * g + 2, qs], in0=o_out[pr], in1=rd[pr], op=ALU.mult),
                         reads=[PB[bO], RDB[sb_]], writes=[OB[n][g][hi]])
        S.barrier()
        self.out_proj(layer, gnext)


_PROG_CACHE = {}


def _get_prog(subs):
    key = tuple(subs)
    if key not in _PROG_CACHE:
        _PROG_CACHE[key] = Prog(subs)
    return _PROG_CACHE[key]


def _run(subs, xT_list, shared):
    prog = _get_prog(subs)
    in_maps = []
    for xT in xT_list:
        m = dict(shared)
        m["xT"] = xT
        in_maps.append(m)
    res = run_bass_kernel_spmd(prog.nc, in_maps, core_ids=list(range(len(xT_list))))
    return [r["yT"] for r in res.results]


def _shared_inputs(inp):
    cosT, sinT = _rope_tables()
    shared = {
        "par": _pack_params(inp["norm_g"], inp["rec_conv_w"], inp["rec_conv_b"], inp["rg_b_r"], inp["rg_b_i"],
                            inp["rg_lambda"], inp["hgrn_lb_logits"], inp["hgrn_norm_g"], inp["att_sinks"]),
        "cst": _make_consts(),
        "cosT": cosT,
        "sinT": sinT,
    }
    for k in ("rec_w_in", "rg_w_r", "rg_w_i", "rec_w_out", "att_w_qkv", "att_w_o", "mlp_w1", "mlp_w2"):
        shared[k] = np.ascontiguousarray(np.asarray(inp[k], np.float32))
    return shared


LAUNCH_GROUPS = [list(range(8))]


def kernel(**inp):
    x = np.asarray(inp["x"], np.float32)
    B = x.shape[0]
    shared = _shared_inputs(inp)
    cur = [np.ascontiguousarray(x[b].T) for b in range(B)]
    for subs in LAUNCH_GROUPS:
        cur = _run(subs, cur, shared)
    out = np.stack([np.ascontiguousarray(c.T) for c in cur], axis=0)
    return out.astype(np.float32)
```
